# Optimizing a Trainium2 kernel written in Bass

```python
import math
import jax, jax.numpy as jnp
from jax import lax
import numpy as np

D_MODEL = 2048
BATCH = 4
SEQ = 2048
DEPTH = 1
DEC_BATCH = 128
DEC_SEQ = 8
PAST_LEN = 16384
PAGE_SIZE = 128

MIX_WIDTH = D_MODEL
LRU_WIDTH = MIX_WIDTH // 2
LRU_BLOCKS = 8
LRU_BLOCK = LRU_WIDTH // LRU_BLOCKS
CONV_WIDTH = 4
LRU_C = 8.0
RWKV_WIDTH = MIX_WIDTH - LRU_WIDTH
RWKV_HEAD = 64
RWKV_HEADS = RWKV_WIDTH // RWKV_HEAD
DECAY_LORA = 64
A_LORA = 64
G_LORA = 160
SHIFT_WIDTH = 3 * RWKV_WIDTH + DECAY_LORA + A_LORA + G_LORA
P_TOTAL = 2 * LRU_WIDTH + SHIFT_WIDTH
RWKV_SPLITS = [RWKV_WIDTH, 2 * RWKV_WIDTH, 3 * RWKV_WIDTH,
               3 * RWKV_WIDTH + DECAY_LORA, 3 * RWKV_WIDTH + DECAY_LORA + A_LORA]
DECAY_SCALE = math.exp(-0.5)
N_MEM = 256
X_HEADS = 4
X_HEAD_DIM = D_MODEL // X_HEADS
D_FF = 4 * D_MODEL
LN_EPS = 1e-5
GN_EPS = 64e-5
ALPHA = (2 * DEPTH) ** 0.25
BETA = (8 * DEPTH) ** -0.25

kernel_name = "hymba_rglru_rwkv7_memxattn_step"


def layer_norm(x, g, b):
    xf = x.astype(jnp.float32)
    mu = jnp.mean(xf, -1, keepdims=True)
    var = jnp.mean(jnp.square(xf - mu), -1, keepdims=True)
    return ((xf - mu) * lax.rsqrt(var + LN_EPS) * g + b).astype(x.dtype)


def causal_conv(u, buf, w, b):
    T = u.shape[1]
    up = jnp.concatenate([buf.astype(u.dtype), u], axis=1)
    out = b + w[0] * up[:, 0:T]
    for j in range(1, CONV_WIDTH):
        out = out + w[j] * up[:, j:j + T]
    return out, up[:, T:]


def rg_lru(xc, h0, wa, ba, wx, bx, lam):
    B, T, _ = xc.shape
    f32 = jnp.float32
    xf = xc.astype(f32)
    xb = xf.reshape(B, T, LRU_BLOCKS, LRU_BLOCK)
    r = jax.nn.sigmoid(jnp.einsum('btnc,ncd->btnd', xb, wa.astype(f32)).reshape(B, T, LRU_WIDTH) + ba)
    i = jax.nn.sigmoid(jnp.einsum('btnc,ncd->btnd', xb, wx.astype(f32)).reshape(B, T, LRU_WIDTH) + bx)
    log_a = -LRU_C * r * jax.nn.softplus(-lam.astype(f32))
    a = jnp.exp(log_a)
    bterm = jnp.sqrt(-jnp.expm1(2.0 * log_a)) * (i * xf)
    bterm = bterm.at[:, 0].add(a[:, 0] * h0.astype(f32))

    def combine(lhs, rhs):
        a1, b1 = lhs
        a2, b2 = rhs
        return a1 * a2, a2 * b1 + b2

    _, h = lax.associative_scan(combine, (a, bterm), axis=1)
    return h, h[:, -1]


def rwkv7_mix(u, shift0, S0, mu, w0, w_up, a0, a_up, g_up, k_k, k_a, r_k, ln_w, ln_b):
    B, T, _ = u.shape
    f32 = jnp.float32
    uf = u.astype(f32)
    prev = jnp.concatenate([shift0.astype(f32)[:, None], uf[:, :-1]], axis=1)
    z = uf + (prev - uf) * mu
    r, k, v, wd, ad, gd = jnp.split(z, RWKV_SPLITS, axis=-1)
    w = jnp.exp(-DECAY_SCALE * jax.nn.sigmoid(w0 + jnp.tanh(wd) @ w_up.astype(f32)))
    a = jax.nn.sigmoid(a0 + ad @ a_up.astype(f32))
    g = jax.nn.sigmoid(gd) @ g_up.astype(f32)

    def heads(t):
        return t.reshape(B, T, RWKV_HEADS, RWKV_HEAD)

    kk = heads(k * k_k)
    kk = kk * lax.rsqrt(jnp.maximum(jnp.sum(kk * kk, -1, keepdims=True), 1e-24))
    k = k * (1.0 + (a - 1.0) * k_a)
    r_h, k_h, v_h, a_h, w_h = heads(r), heads(k), heads(v), heads(a), heads(w)

    def step(S, inp):
        r_t, w_t, k_t, v_t, kk_t, a_t = inp
        s_kk = jnp.einsum('bhij,bhj->bhi', S, kk_t)
        S = (S * w_t[:, :, None, :]
             - s_kk[..., None] * (kk_t * a_t)[:, :, None, :]
             + v_t[..., None] * k_t[:, :, None, :])
        return S, jnp.einsum('bhij,bhj->bhi', S, r_t)

    seq = tuple(jnp.swapaxes(t, 0, 1) for t in (r_h, w_h, k_h, v_h, kk, a_h))
    S_T, o = lax.scan(step, S0.astype(f32), seq)
    o = jnp.swapaxes(o, 0, 1)
    mean = jnp.mean(o, -1, keepdims=True)
    var = jnp.mean(jnp.square(o - mean), -1, keepdims=True)
    o = (o - mean) * lax.rsqrt(var + GN_EPS) * ln_w.reshape(RWKV_HEADS, RWKV_HEAD) + ln_b.reshape(RWKV_HEADS, RWKV_HEAD)
    bonus = jnp.sum(r_h * k_h * r_k, -1, keepdims=True) * v_h
    y = (o + bonus).reshape(B, T, RWKV_WIDTH) * g
    return y, S_T, u[:, -1]


def hybrid_layer(x, mem_k, mem_v, h0, conv0, S0, shift0,
                 w_in, lru_conv_w, lru_conv_b, lru_wa, lru_ba, lru_wx, lru_bx, lru_L,
                 rwkv_mu, rwkv_w0, rwkv_w_up, rwkv_a0, rwkv_a_up, rwkv_g_up,
                 rwkv_k_k, rwkv_k_a, rwkv_r_k, rwkv_ln_w, rwkv_ln_b,
                 w_out, ln1_g, ln1_b, xa_wq, xa_wo, ln2_g, ln2_b,
                 mlp_w1, mlp_w2, ln3_g, ln3_b):
    B, T, _ = x.shape
    dt = x.dtype
    p = x @ w_in
    u_x = p[..., :LRU_WIDTH]
    u_gate = p[..., LRU_WIDTH:2 * LRU_WIDTH]
    u_rwkv = p[..., 2 * LRU_WIDTH:]
    xc, conv_new = causal_conv(u_x, conv0, lru_conv_w, lru_conv_b)
    h, h_last = rg_lru(xc, h0, lru_wa, lru_ba, lru_wx, lru_bx, lru_L)
    y_lru = h.astype(dt) * jax.nn.gelu(u_gate)
    y_rwkv, S_new, shift_new = rwkv7_mix(u_rwkv, shift0, S0, rwkv_mu, rwkv_w0, rwkv_w_up, rwkv_a0,
                                         rwkv_a_up, rwkv_g_up, rwkv_k_k, rwkv_k_a, rwkv_r_k,
                                         rwkv_ln_w, rwkv_ln_b)
    mix = jnp.concatenate([y_lru, y_rwkv.astype(dt)], axis=-1) @ w_out
    x = layer_norm(ALPHA * x + mix, ln1_g, ln1_b)
    q = (x @ xa_wq).reshape(B, T, X_HEADS, X_HEAD_DIM)
    s = jnp.einsum('bthd,bmhd->bhtm', q, mem_k.astype(dt)).astype(jnp.float32) * (X_HEAD_DIM ** -0.5)
    pr = jax.nn.softmax(s, axis=-1).astype(dt)
    att = jnp.einsum('bhtm,bmhd->bthd', pr, mem_v.astype(dt)).reshape(B, T, D_MODEL) @ xa_wo
    x = layer_norm(ALPHA * x + att, ln2_g, ln2_b)
    ff = jnp.square(jax.nn.relu(x @ mlp_w1)) @ mlp_w2
    x = layer_norm(ALPHA * x + ff, ln3_g, ln3_b)
    return x, h_last.astype(dt), conv_new.astype(dt), S_new.astype(dt), shift_new.astype(dt)


def setup_inputs(seed: int = 0) -> dict:
    key = jax.random.key(seed)
    ks = iter(jax.random.split(key, 64))
    f32 = jnp.float32

    def nrm(shape, s):
        return jax.random.normal(next(ks), shape, f32) * s

    def unif(shape, lo, hi):
        return jax.random.uniform(next(ks), shape, f32, minval=lo, maxval=hi)

    Lr = DEPTH
    u = unif((Lr, LRU_WIDTH), 0.9, 0.999)
    a_base = u ** (1.0 / LRU_C)
    lru_L = jnp.log(a_base) - jnp.log1p(-a_base)
    return {
        "x_prompt": nrm((BATCH, SEQ, D_MODEL), 1.0),
        "x_sample": nrm((DEC_BATCH, DEC_SEQ, D_MODEL), 1.0),
        "mem_prompt": nrm((BATCH, N_MEM, D_MODEL), 1.0),
        "cache_mem_k": nrm((Lr, DEC_BATCH, N_MEM, X_HEADS, X_HEAD_DIM), 1.0),
        "cache_mem_v": nrm((Lr, DEC_BATCH, N_MEM, X_HEADS, X_HEAD_DIM), BETA),
        "state_lru_h": nrm((Lr, DEC_BATCH, LRU_WIDTH), 0.5),
        "state_lru_conv": nrm((Lr, DEC_BATCH, CONV_WIDTH - 1, LRU_WIDTH), 1.0),
        "state_rwkv_S": nrm((Lr, DEC_BATCH, RWKV_HEADS, RWKV_HEAD, RWKV_HEAD), 0.3),
        "state_rwkv_shift": nrm((Lr, DEC_BATCH, SHIFT_WIDTH), 1.0),
        "w_in": nrm((Lr, D_MODEL, P_TOTAL), D_MODEL ** -0.5),
        "lru_conv_w": nrm((Lr, CONV_WIDTH, LRU_WIDTH), CONV_WIDTH ** -0.5),
        "lru_conv_b": nrm((Lr, LRU_WIDTH), 0.01),
        "lru_wa": nrm((Lr, LRU_BLOCKS, LRU_BLOCK, LRU_BLOCK), LRU_BLOCK ** -0.5),
        "lru_ba": nrm((Lr, LRU_WIDTH), 0.01),
        "lru_wx": nrm((Lr, LRU_BLOCKS, LRU_BLOCK, LRU_BLOCK), LRU_BLOCK ** -0.5),
        "lru_bx": nrm((Lr, LRU_WIDTH), 0.01),
        "lru_L": lru_L,
        "rwkv_mu": unif((Lr, SHIFT_WIDTH), 0.0, 1.0),
        "rwkv_w0": unif((Lr, RWKV_WIDTH), -6.0, 1.0),
        "rwkv_w_up": nrm((Lr, DECAY_LORA, RWKV_WIDTH), 0.5 * DECAY_LORA ** -0.5),
        "rwkv_a0": nrm((Lr, RWKV_WIDTH), 0.1),
        "rwkv_a_up": nrm((Lr, A_LORA, RWKV_WIDTH), 0.5 * A_LORA ** -0.5),
        "rwkv_g_up": nrm((Lr, G_LORA, RWKV_WIDTH), G_LORA ** -0.5),
        "rwkv_k_k": 0.85 + nrm((Lr, RWKV_WIDTH), 0.05),
        "rwkv_k_a": 1.0 + nrm((Lr, RWKV_WIDTH), 0.05),
        "rwkv_r_k": nrm((Lr, RWKV_HEADS, RWKV_HEAD), 0.1),
        "rwkv_ln_w": 1.0 + nrm((Lr, RWKV_WIDTH), 0.05),
        "rwkv_ln_b": nrm((Lr, RWKV_WIDTH), 0.02),
        "w_out": nrm((Lr, MIX_WIDTH, D_MODEL), MIX_WIDTH ** -0.5 * BETA),
        "ln1_g": 1.0 + nrm((Lr, D_MODEL), 0.02),
        "ln1_b": nrm((Lr, D_MODEL), 0.02),
        "xa_wq": nrm((Lr, D_MODEL, D_MODEL), D_MODEL ** -0.5),
        "xa_wk": nrm((Lr, D_MODEL, D_MODEL), D_MODEL ** -0.5),
        "xa_wv": nrm((Lr, D_MODEL, D_MODEL), D_MODEL ** -0.5 * BETA),
        "xa_wo": nrm((Lr, D_MODEL, D_MODEL), D_MODEL ** -0.5 * BETA),
        "ln2_g": 1.0 + nrm((Lr, D_MODEL), 0.02),
        "ln2_b": nrm((Lr, D_MODEL), 0.02),
        "mlp_w1": nrm((Lr, D_MODEL, D_FF), D_MODEL ** -0.5),
        "mlp_w2": nrm((Lr, D_FF, D_MODEL), D_FF ** -0.5 * BETA),
        "ln3_g": 1.0 + nrm((Lr, D_MODEL), 0.02),
        "ln3_b": nrm((Lr, D_MODEL), 0.02),
    }


def reference(x_prompt, x_sample, mem_prompt, cache_mem_k, cache_mem_v, state_lru_h, state_lru_conv,
              state_rwkv_S, state_rwkv_shift,
              w_in, lru_conv_w, lru_conv_b, lru_wa, lru_ba, lru_wx, lru_bx, lru_L,
              rwkv_mu, rwkv_w0, rwkv_w_up, rwkv_a0, rwkv_a_up, rwkv_g_up, rwkv_k_k, rwkv_k_a, rwkv_r_k,
              rwkv_ln_w, rwkv_ln_b,
              w_out, ln1_g, ln1_b, xa_wq, xa_wk, xa_wv, xa_wo, ln2_g, ln2_b,
              mlp_w1, mlp_w2, ln3_g, ln3_b):
    B = x_prompt.shape[0]
    dt = x_prompt.dtype
    yp, ys = x_prompt, x_sample
    mk_p, mv_p, h_p, c_p, S_p, sh_p = [], [], [], [], [], []
    h_s, c_s, S_s, sh_s = [], [], [], []
    for l in range(DEPTH):
        wl = [t[l] for t in (w_in, lru_conv_w, lru_conv_b, lru_wa, lru_ba, lru_wx, lru_bx, lru_L,
                             rwkv_mu, rwkv_w0, rwkv_w_up, rwkv_a0, rwkv_a_up, rwkv_g_up,
                             rwkv_k_k, rwkv_k_a, rwkv_r_k, rwkv_ln_w, rwkv_ln_b,
                             w_out, ln1_g, ln1_b, xa_wq, xa_wo, ln2_g, ln2_b,
                             mlp_w1, mlp_w2, ln3_g, ln3_b)]
        mem_k = (mem_prompt @ xa_wk[l]).reshape(B, N_MEM, X_HEADS, X_HEAD_DIM)
        mem_v = (mem_prompt @ xa_wv[l]).reshape(B, N_MEM, X_HEADS, X_HEAD_DIM)
        yp, hl, cl, Sl, shl = hybrid_layer(
            yp, mem_k, mem_v,
            jnp.zeros((B, LRU_WIDTH), dt), jnp.zeros((B, CONV_WIDTH - 1, LRU_WIDTH), dt),
            jnp.zeros((B, RWKV_HEADS, RWKV_HEAD, RWKV_HEAD), dt), jnp.zeros((B, SHIFT_WIDTH), dt),
            *wl)
        mk_p.append(mem_k); mv_p.append(mem_v); h_p.append(hl); c_p.append(cl); S_p.append(Sl); sh_p.append(shl)
        ys, hl, cl, Sl, shl = hybrid_layer(
            ys, cache_mem_k[l], cache_mem_v[l],
            state_lru_h[l], state_lru_conv[l], state_rwkv_S[l], state_rwkv_shift[l],
            *wl)
        h_s.append(hl); c_s.append(cl); S_s.append(Sl); sh_s.append(shl)
    mem_k_prompt = jnp.stack(mk_p)
    mem_v_prompt = jnp.stack(mv_p)
    lru_h_prompt = jnp.stack(h_p)
    lru_conv_prompt = jnp.stack(c_p)
    rwkv_S_prompt = jnp.stack(S_p)
    rwkv_shift_prompt = jnp.stack(sh_p)
    lru_h_sample = jnp.stack(h_s)
    lru_conv_sample = jnp.stack(c_s)
    rwkv_S_sample = jnp.stack(S_s)
    rwkv_shift_sample = jnp.stack(sh_s)
    return (yp, ys, mem_k_prompt, mem_v_prompt, lru_h_prompt, lru_conv_prompt, rwkv_S_prompt,
            rwkv_shift_prompt, lru_h_sample, lru_conv_sample, rwkv_S_sample, rwkv_shift_sample)
```

```python
import math
import numpy as np
import concourse.bass as bass
import concourse.mybir as mybir
from concourse.bass_utils import run_bass_kernel_spmd

F32 = mybir.dt.float32
BF16 = mybir.dt.bfloat16
AF = mybir.ActivationFunctionType
ALU = mybir.AluOpType

ENGINES = ("pe", "act", "dve", "pool", "sp")


def _is_psum_key(k):
    if isinstance(k, tuple):
        if k[0] in ("ps", "pj", "psH", "psU"):
            return True
    if isinstance(k, tuple):
        return False
    return isinstance(k, str) and k.startswith("p_")


class Sched:
    def __init__(self, nc, max_dma_streams=90):
        self.nc = nc
        self.q = {e: [] for e in ENGINES}
        self.sems = {}
        for e in ("pe", "act", "dve", "pool"):
            self.sems["E_" + e] = nc.alloc_semaphore("sem_" + e)
        self.val = {k: 0 for k in self.sems}
        self.seen = {e: {} for e in ENGINES}
        self.last_w = {}
        self.readers = {}
        self.max_dma_streams = max_dma_streams
        self.n_inst = 0

    def _stream_sem(self, stream):
        k = "D_" + stream
        if k not in self.sems:
            assert sum(1 for s in self.sems if s.startswith("D_")) < self.max_dma_streams, "too many dma streams"
            self.sems[k] = self.nc.alloc_semaphore("dsem_" + stream)
            self.val[k] = 0
        return k

    def op(self, eng, fns, reads=(), writes=(), dma=None):
        if callable(fns):
            fns = [fns]
        deps = {}

        def need(tok):
            if tok is None:
                return
            k, v = tok
            if k == "E_pe" and eng == "pe":
                return
            if self.seen[eng].get(k, 0) >= v:
                return
            if deps.get(k, 0) < v:
                deps[k] = v

        xr = [r for r in reads if _is_psum_key(r)]
        if xr:
            writes = list(writes) + [r for r in xr if r not in writes]
        own = "E_" + eng
        for r in reads:
            need(self.last_w.get(r))
        for w in writes:
            tok = self.last_w.get(w)
            if tok is not None and not (tok[0] == own and w not in reads):
                need(tok)
            for tok in self.readers.get(w, {}).items():
                need(tok)
        if dma is not None:
            k = self._stream_sem(eng + "_" + dma)
            if self.val[k] > 0:
                need((k, self.val[k]))
            self.val[k] += 16
            tok = (k, self.val[k])
            inc = (k, 16)
        else:
            k = "E_" + eng
            self.val[k] += 1
            tok = (k, self.val[k])
            inc = (k, 1)
        for k2, v2 in deps.items():
            self.seen[eng][k2] = v2
        self.q[eng].append((fns, list(deps.items()), inc))
        for w in writes:
            self.last_w[w] = tok
            self.readers[w] = {}
        for r in reads:
            d = self.readers.setdefault(r, {})
            if d.get(tok[0], 0) < tok[1]:
                d[tok[0]] = tok[1]
        self.n_inst += len(fns)
        return tok

    def barrier(self):
        for eng in ENGINES:
            deps = [(k, v) for k, v in self.val.items() if v > 0 and self.seen[eng].get(k, 0) < v
                    and not (k == "E_pe" and eng == "pe")]
            for k, v in deps:
                self.seen[eng][k] = v
            if deps:
                self.q[eng].append(([], deps, None))

    def final_wait(self, eng="sp"):
        deps = [(k, v) for k, v in self.val.items() if v > 0]
        self.q[eng].append(([], deps, None))

    def emit(self):
        nc, sems, q = self.nc, self.sems, self.q

        def run(e, lst):
            for fns, waits, inc in lst:
                for k, v in waits:
                    e.wait_ge(sems[k], v)
                inst = None
                for f in fns:
                    inst = f(e)
                if inc is not None:
                    inst.then_inc(sems[inc[0]], inc[1])

        with nc.Block() as blk:
            @blk.tensor
            def _(e):
                run(e, q["pe"])

            @blk.scalar
            def _(e):
                run(e, q["act"])

            @blk.vector
            def _(e):
                run(e, q["dve"])

            @blk.gpsimd
            def _(e):
                run(e, q["pool"])

            @blk.sync
            def _(e):
                run(e, q["sp"])


class Arena:
    def __init__(self, nc, base, top):
        self.nc, self.base, self.top, self.cur, self.n, self.peak = nc, base, top, base, 0, base

    def alloc(self, name, shape, dtype, align=64):
        size = 1
        for s in shape[1:]:
            size *= s
        nbytes = size * mybir.dt.size(dtype)
        off = (self.cur + align - 1) // align * align
        assert off + nbytes <= self.top, f"SBUF overflow allocating {name}: {off}+{nbytes} > {self.top}"
        self.cur = off + nbytes
        self.peak = max(self.peak, self.cur)
        self.n += 1
        return self.nc.alloc_sbuf_tensor_at(f"{name}_{self.n}", list(shape), dtype, offset=off)

    def mark(self):
        return self.cur

    def release(self, m):
        self.cur = m


def bcast(src_ap, dims):
    return bass.AP(src_ap.tensor, src_ap.offset, [list(src_ap.ap[0])] + [list(d) for d in dims])


D = 2048
KC = 16
T_P = 2048
NPRE = 1024
NOWN = 1024
NSEQ = 16
TS = 8
NS = NSEQ * TS
NTOK = NPRE + NOWN + NS
NOT = NOWN + NS
LRU_W = 1024
NPAIR = 8
P_TOTAL = 5408
N_WT = 43
EW = 2320
NE = 2193
SB0 = 2049
LW = 2240
DECAY_C = math.exp(-0.5)
CH = 64
ALPHA = 2.0 ** 0.25
LN_EPS = 1e-5
GN_EPS = 64e-5
TBLK = [(0, 512), (512, 512), (1024, 512), (1536, 512), (2048, 128)]
EBLK = [(0, 512), (512, 512), (1024, 512), (1536, 512), (2048, 256)]

V_CW, V_CB, V_BA, V_BX, V_LL, V_MU = 0, 32, 40, 48, 56, 64
V_W0, V_A0, V_KK, V_KA, V_RK, V_LNW, V_LNB = 91, 99, 107, 115, 123, 131, 139
V_LN1G, V_LN1B, V_LN2G, V_LN2B, V_LN3G, V_LN3B = 147, 163, 179, 195, 211, 227
V_FLAG = 243
NV = 244
DV_CL, DV_CL2, DV_OMKA = 0, 8, 16
NDV = 24


def I(method, *a, **kw):
    return lambda e: getattr(e, method)(*a, **kw)


class Ctx:
    pass


def build(debug=False, stop_after=None):
    nc = bass.Bass("TRN2", target_bir_lowering=False)
    S = Sched(nc)
    A = Arena(nc, 16512, 229300)
    C = Ctx()
    C.nc, C.S, C.A, C.debug = nc, S, A, debug
    build.last_C = C

    def din(name, shape, dt=F32):
        return nc.dram_tensor(name, list(shape), dt, kind="ExternalInput")

    def dout(name, shape, dt=F32):
        return nc.dram_tensor(name, list(shape), dt, kind="ExternalOutput")

    def dscr(name, shape, dt=F32):
        return nc.dram_tensor(name, list(shape), dt, kind=("ExternalOutput" if debug else "Internal"))

    C.din, C.dout, C.dscr = din, dout, dscr

    xT_h = din("xT", [D, NTOK])
    win_h = din("w_in_t", [N_WT, 128, KC, 128])
    vec_h = din("vec", [128, NV])
    cmat_h = din("cmat", [128, 384])
    shiftT_h = din("shiftT", [128, 27, NSEQ])
    convT_h = din("convT", [128, 8, NSEQ, 3])
    lruhT_h = din("lruhT", [128, 8, NSEQ])
    lora_h = din("lora_wa", [128, 1024])
    gup_h = din("gup", [160, 1024])
    lruw_h = din("lru_w", [2, 8, 128, 128])
    C.xT_h = xT_h

    shn_o = dout("shn", [128, 27, 1 + NSEQ])
    lruh_o = dout("lruh", [128, 8, 1 + NSEQ])
    lruc_o = dout("lruc", [128, 8, 1 + NSEQ, 3])

    C.L1_s = L1_s = dscr("L1_s", [128, NPAIR, EW, 4], BF16)
    C.w_s = w_s = dscr("w_s", [128, NPAIR, EW])
    C.bk_s = bk_s = dscr("bk_s", [2, 2, NPAIR, EW, 128], BF16)
    C.v_s = v_s = dscr("v_s", [2, NPAIR, EW, 64], BF16)
    C.G_s = G_s = dscr("G_s", [2, 128, NPAIR, EW])
    C.yT_s = yT_s = dscr("yT_s", [16, 128, NOT], BF16)

    vec = A.alloc("vec", [128, NV], F32)
    dv = A.alloc("dv", [128, NDV], F32)
    cmat = A.alloc("cmat", [128, 384], F32)
    identb = A.alloc("identb", [128, 128], BF16)
    C.vec, C.dv, C.cmat, C.identb = vec, dv, cmat, identb
    C.ident = ident = cmat[:, 0:128]
    C.blk1 = blk1 = cmat[:, 128:256]
    C.ones = ones = cmat[:, 256:384]

    S.op("sp", I("dma_start", out=vec[:], in_=vec_h.ap()), writes=["vec"], dma="c0")
    S.op("sp", I("dma_start", out=cmat[:], in_=cmat_h.ap()), writes=["cmat"], dma="c1")
    S.op("dve", I("tensor_copy", out=identb[:], in_=cmat[:, 0:128]), reads=["cmat"], writes=["identb"])
    S.op("act", I("activation", out=dv[:, 0:8], in_=vec[:, V_LL:V_LL + 8], func=AF.Exp, scale=-1.0), reads=["vec"], writes=["dv"])
    S.op("act", I("activation", out=dv[:, 0:8], in_=dv[:, 0:8], func=AF.Ln, bias=1.0), reads=["dv"], writes=["dv"])
    S.op("dve", I("tensor_scalar", out=dv[:, 8:16], in0=dv[:, 0:8], scalar1=-16.0, scalar2=None, op0=ALU.mult), reads=["dv"], writes=["dv"])
    S.op("dve", I("tensor_scalar", out=dv[:, 0:8], in0=dv[:, 0:8], scalar1=-8.0, scalar2=None, op0=ALU.mult), reads=["dv"], writes=["dv"])
    S.op("dve", I("tensor_scalar", out=dv[:, 16:24], in0=vec[:, V_KA:V_KA + 8], scalar1=-1.0, scalar2=1.0, op0=ALU.mult, op1=ALU.add),
         reads=["vec"], writes=["dv"])

    def vcol(i):
        return vec[:, i:i + 1]

    def dvcol(i):
        return dv[:, i:i + 1]

    C.vcol, C.dvcol = vcol, dvcol
    C.PS = PS = [nc.alloc_psum_tensor(f"ps{i}", [128, 512], F32) for i in range(8)]

    m_phase1 = A.mark()
    xb = A.alloc("xb", [128, KC, NTOK], BF16)
    xT_v = xT_h.ap().rearrange("(kc kp) t -> kp kc t", kp=128)
    def load_xb(bi):
        c0, n = TBLK[bi]
        S.op("pool", I("dma_start", out=xb[:, :, c0:c0 + n], in_=xT_v[:, :, c0:c0 + n]), writes=[("xb", bi)], dma=f"xb{bi % 2}")

    NWS = 4
    wslot = [A.alloc(f"win{i}", [128, KC, 128], BF16) for i in range(NWS)]
    wt_loaded = {}
    wt_order = [40, 41, 42]
    for p in range(NPAIR):
        wt_order += [16 + p, 24 + p, 32 + p]
    for c in range(8):
        wt_order += [c, 8 + c]
    wt_issue = [0]

    def issue_wload():
        i = wt_issue[0]
        if i >= len(wt_order):
            return
        wt = wt_order[i]
        sl = i % NWS
        S.op("pool", I("dma_start", out=wslot[sl][:], in_=win_h.ap()[wt]), writes=[("win", sl)], dma=f"win{sl}")
        wt_loaded[wt] = sl
        wt_issue[0] += 1

    issue_wload()
    load_xb(0)
    load_xb(1)
    issue_wload()
    issue_wload()
    load_xb(2)
    load_xb(3)
    load_xb(4)

    pj_rot = [0]

    def project(wt, evac, blocks=(0, 1, 2, 3, 4)):
        issue_wload()
        sl = wt_loaded[wt]
        for bi in blocks:
            c0, n = TBLK[bi]
            b = pj_rot[0] % 2
            pj_rot[0] += 1
            pt = PS[b]
            fns = [I("matmul", pt[:, 0:n], lhsT=wslot[sl][:, kc, :], rhs=xb[:, kc, c0:c0 + n], start=(kc == 0), stop=(kc == KC - 1))
                   for kc in range(KC)]
            S.op("pe", fns, reads=[("win", sl), ("xb", bi)], writes=[("pj", b)])
            evac(bi, pt, b, c0, n)

    def project_deferred(wt, blocks, banks):
        issue_wload()
        sl = wt_loaded[wt]
        out = []
        for bi, bk_ in zip(blocks, banks):
            c0, n = TBLK[bi]
            pt = PS[bk_]
            fns = [I("matmul", pt[:, 0:n], lhsT=wslot[sl][:, kc, :], rhs=xb[:, kc, c0:c0 + n], start=(kc == 0), stop=(kc == KC - 1))
                   for kc in range(KC)]
            S.op("pe", fns, reads=[("win", sl), ("xb", bi)], writes=[("pj", bk_)])
            out.append((bi, pt, bk_, c0, n))
        return out

    m_rwkv = A.mark()
    shiftT = A.alloc("shiftT", [128, 27, NSEQ], F32)
    S.op("sp", I("dma_start", out=shiftT[:], in_=shiftT_h.ap()), writes=["shiftT"], dma="c2")
    shn = A.alloc("shn", [128, 27, 1 + NSEQ], F32)
    lora = A.alloc("lora", [128, 1024], BF16)
    gup1 = A.alloc("gup1", [128, 1024], BF16)
    gup2 = A.alloc("gup2", [32, 1024], BF16)
    S.op("pool", I("dma_start", out=lora[:], in_=lora_h.ap()), writes=["lora"], dma="c3")
    S.op("pool", I("dma_start", out=gup1[:], in_=gup_h.ap()[0:128]), writes=["gup1"], dma="c4")
    S.op("pool", I("dma_start", out=gup2[:], in_=gup_h.ap()[128:160]), writes=["gup2"], dma="c5")
    codes = [A.alloc(f"codes{i}", [128, EW], BF16) for i in range(3)]
    Usets = [[A.alloc(f"U{s_}_{i}", [128, EW], F32) for i in range(3)] for s_ in range(2)]
    DW = 580
    dtmp = A.alloc("dtmp", [128, DW], F32)
    for s_ in range(2):
        for i in range(3):
            S.op("pool", I("memset", Usets[s_][i][:], 0.0), writes=[("U", s_, i)])
    cur = {"set": 0}

    def evac_rwkv(ui, st=0):
        def f(bi, pt, b, c0, n):
            u = Usets[st][ui]
            if bi < 4:
                S.op("act", I("activation", out=u[:, 1 + c0:1 + c0 + n], in_=pt[:, 0:n], func=AF.Copy),
                     reads=[("pj", b)], writes=[("U", st, ui)])
            else:
                dst = bcast(u[:, SB0 + 1:SB0 + 2], [[9, NSEQ], [1, TS]])
                S.op("act", I("activation", out=dst, in_=pt[:, 0:NS].rearrange("p (g s) -> p g s", s=TS), func=AF.Copy),
                     reads=[("pj", b)], writes=[("U", st, ui)])
        return f

    def shift_ops(ui, ti, st=0):
        u = Usets[st][ui]
        key = ("U", st, ui)
        dst = bcast(u[:, SB0:SB0 + 1], [[9, NSEQ]])
        S.op("pool", I("tensor_copy", out=dst, in_=shiftT[:, ti, :]), reads=["shiftT", key], writes=[key]); yield
        S.op("pool", I("tensor_copy", out=shn[:, ti, 0:1], in_=u[:, 2048:2049]), reads=[key], writes=["shn"]); yield
        src = bcast(u[:, SB0 + TS:SB0 + TS + 1], [[9, NSEQ]])
        S.op("pool", I("tensor_copy", out=shn[:, ti, 1:1 + NSEQ], in_=src), reads=[key], writes=["shn"]); yield
        for c1 in range(EW, 1, -DW):
            c0 = max(1, c1 - DW)
            n = c1 - c0
            S.op("pool", I("tensor_tensor", out=dtmp[:, 0:n], in0=u[:, c0 - 1:c1 - 1], in1=u[:, c0:c1], op=ALU.subtract),
                 reads=[key], writes=["dtmp"]); yield
            S.op("dve", I("scalar_tensor_tensor", out=u[:, c0:c1], in0=dtmp[:, 0:n], scalar=vcol(V_MU + ti), in1=u[:, c0:c1],
                          op0=ALU.mult, op1=ALU.add), reads=["dtmp", key, "vec"], writes=[key]); yield

    def shift_tile(ui, ti, st=0):
        for _ in shift_ops(ui, ti, st):
            pass

    project(40, evac_rwkv(0))
    shift_tile(0, 24)
    S.op("act", I("activation", out=codes[0][0:64, :], in_=Usets[0][0][0:64, :], func=AF.Tanh), reads=[("U", 0, 0)], writes=["codes0"])
    S.op("act", I("activation", out=codes[0][64:128, :], in_=Usets[0][0][64:128, :], func=AF.Copy), reads=[("U", 0, 0)], writes=["codes0"])
    project(41, evac_rwkv(1))
    shift_tile(1, 25)
    S.op("act", I("activation", out=codes[1][:], in_=Usets[0][1][:], func=AF.Sigmoid), reads=[("U", 0, 1)], writes=["codes1"])
    project(42, evac_rwkv(2))
    shift_tile(2, 26)
    S.op("act", I("activation", out=codes[2][:], in_=Usets[0][2][:], func=AF.Sigmoid), reads=[("U", 0, 2)], writes=["codes2"])

    BW = 128
    NPAR = 4

    def mk_temps(k):
        T = {}
        for nm in ("tA", "tW", "tK", "tQ", "tT", "tR", "tG1", "tG2", "tL", "tC", "tP"):
            T[nm] = A.alloc(f"{nm}{k}", [128, BW], F32)
        for nm in ("bB", "bK", "bV"):
            T[nm] = A.alloc(f"{nm}{k}", [128, BW], BF16)
        T["L1st"] = A.alloc(f"L1st{k}", [128, BW, 4], BF16)
        T["stBK"] = A.alloc(f"stBK{k}", [128, 2, 2, 128], BF16)
        T["stV"] = A.alloc(f"stV{k}", [128, 128], BF16)
        S.op("pool", I("memset", T["L1st"][:], 0.0), writes=[("L1st", k)])
        S.op("pool", I("memset", T["stBK"][:], 0.0), writes=[("stBK", k)])
        return T

    TT = [mk_temps(k) for k in range(NPAR)]
    ccar = A.alloc("ccar", [128, 1], F32)
    keepE = A.alloc("keepE", [128, EW], BF16)
    S.op("pool", I("memset", keepE[:], 1.0), writes=["keepE"])
    S.op("pool", I("memset", bcast(keepE[:, 0:1], [[CH, 2048 // CH + 1]]), 0.0), reads=["keepE"], writes=["keepE"])
    S.op("pool", I("memset", bcast(keepE[:, SB0:SB0 + 1], [[9, NSEQ]]), 0.0), reads=["keepE"], writes=["keepE"])
    S.op("pool", I("memset", bcast(keepE[:, SB0 + TS:SB0 + TS + 1], [[9, NSEQ]]), 0.0), reads=["keepE"], writes=["keepE"])
    P_WL, P_AL, P_GL, P_SS, P_TR = PS[2], PS[3], PS[4], PS[5], PS[6]
    P_TRb = P_TR.bitcast(BF16)
    P_TRb2 = PS[7].bitcast(BF16)
    SUBS = [(BW * i, BW) for i in range(2304 // BW)]

    def block_gen(p, e0, bw, k, st):
        T = TT[k]
        tA, tW, tK, tQ, tT, tR, tG1, tG2, tL, tC, tP = (T[n] for n in ("tA", "tW", "tK", "tQ", "tT", "tR", "tG1", "tG2", "tL", "tC", "tP"))
        bB, bK, bV, L1st, stBK, stV = (T[n] for n in ("bB", "bK", "bV", "L1st", "stBK", "stV"))
        K = lambda nm: (nm, k)
        Ur, Uk, Uv = Usets[st]
        ch = slice(128 * p, 128 * p + 128)
        sf = slice(e0 + 1, e0 + 1 + bw)
        ef = slice(e0, e0 + bw)
        own = e0 >= 1024
        w_ = slice(0, bw)
        pc = slice(BW * k, BW * k + bw)
        S.op("pe", I("matmul", P_WL[:, pc], lhsT=lora[0:64, ch], rhs=codes[0][0:64, sf], start=True, stop=True),
             reads=["lora", "codes0"], writes=["p_wl"]); yield
        S.op("pe", I("matmul", P_AL[:, pc], lhsT=lora[64:128, ch], rhs=codes[0][64:128, sf], start=True, stop=True),
             reads=["lora", "codes0"], writes=["p_al"]); yield
        if own:
            S.op("pe", [I("matmul", P_GL[:, pc], lhsT=gup1[:, ch], rhs=codes[1][:, sf], start=True, stop=False, skip_group_check=True),
                        I("matmul", P_GL[:, pc], lhsT=gup2[0:32, ch], rhs=codes[2][0:32, sf], start=False, stop=True, skip_group_check=True)],
                 reads=["gup1", "gup2", "codes1", "codes2"], writes=["p_gl"]); yield
        S.op("act", I("activation", out=tA[:, w_], in_=P_AL[:, pc], func=AF.Sigmoid, bias=vcol(V_A0 + p)), reads=["p_al", "vec"], writes=[K("tA")]); yield
        S.op("act", I("activation", out=tW[:, w_], in_=P_WL[:, pc], func=AF.Sigmoid, bias=vcol(V_W0 + p)), reads=["p_wl", "vec"], writes=[K("tW")]); yield
        S.op("dve", I("tensor_scalar", out=tL[:, w_], in0=tW[:, w_], scalar1=-DECAY_C, scalar2=None, op0=ALU.mult), reads=[K("tW")], writes=[K("tL")]); yield
        carry_in = (e0 == 2176)
        S.op("dve", I("tensor_tensor_scan", out=tC[:, w_], data0=keepE[:, ef], data1=tL[:, w_], initial=(ccar[:, 0:1] if carry_in else 0.0),
                      op0=ALU.mult, op1=ALU.add), reads=["keepE", K("tL")] + (["ccar"] if carry_in else []), writes=[K("tC")]); yield
        if e0 == 2048:
            S.op("dve", I("tensor_copy", out=ccar[:, 0:1], in_=tC[:, bw - 1:bw]), reads=[K("tC")], writes=["ccar"]); yield
        S.op("pool", I("tensor_tensor", out=tP[:, w_], in0=tC[:, w_], in1=tL[:, w_], op=ALU.subtract), reads=[K("tC"), K("tL")], writes=[K("tP")]); yield
        S.op("act", I("activation", out=tP[:, w_], in_=tP[:, w_], func=AF.Exp), reads=[K("tP")], writes=[K("tP")]); yield
        S.op("act", I("activation", out=tW[:, w_], in_=tC[:, w_], func=AF.Exp), reads=[K("tC")], writes=[K("tW")]); yield
        S.op("act", I("activation", out=tC[:, w_], in_=tC[:, w_], func=AF.Exp, scale=-1.0), reads=[K("tC")], writes=[K("tC")]); yield
        S.op("sp", I("dma_start", out=w_s.ap()[:, p, e0:e0 + bw], in_=tW[:, w_]), reads=[K("tW")], writes=["w_s"], dma=f"ws{k}"); yield
        S.op("dve", I("tensor_scalar", out=tK[:, w_], in0=Uk[:, sf], scalar1=vcol(V_KK + p), scalar2=None, op0=ALU.mult),
             reads=[("U", st, 1), "vec"], writes=[K("tK")]); yield
        S.op("act", I("activation", out=tQ[:, w_], in_=tK[:, w_], func=AF.Square), reads=[K("tK")], writes=[K("tQ")]); yield
        S.op("pe", I("matmul", P_SS[:, pc], lhsT=blk1, rhs=tQ[:, w_], start=True, stop=True), reads=["cmat", K("tQ")], writes=["p_ss"]); yield
        S.op("dve", I("tensor_scalar", out=tQ[:, w_], in0=P_SS[:, pc], scalar1=1e-24, scalar2=None, op0=ALU.max), reads=["p_ss"], writes=[K("tQ")]); yield
        S.op("act", I("activation", out=tQ[:, w_], in_=tQ[:, w_], func=AF.Sqrt), reads=[K("tQ")], writes=[K("tQ")]); yield
        S.op("dve", I("reciprocal", out=tQ[:, w_], in_=tQ[:, w_]), reads=[K("tQ")], writes=[K("tQ")]); yield
        S.op("dve", I("tensor_tensor", out=tK[:, w_], in0=tK[:, w_], in1=tQ[:, w_], op=ALU.mult), reads=[K("tK"), K("tQ")], writes=[K("tK")]); yield
        S.op("dve", I("tensor_tensor", out=tQ[:, w_], in0=tK[:, w_], in1=tA[:, w_], op=ALU.mult), reads=[K("tK"), K("tA")], writes=[K("tQ")]); yield
        S.op("pool", I("tensor_tensor", out=bB[:, w_], in0=tQ[:, w_], in1=tC[:, w_], op=ALU.mult), reads=[K("tQ"), K("tC")], writes=[K("bB")]); yield
        S.op("dve", I("tensor_scalar", out=tT[:, w_], in0=tA[:, w_], scalar1=vcol(V_KA + p), scalar2=dvcol(DV_OMKA + p), op0=ALU.mult, op1=ALU.add),
             reads=[K("tA"), "vec", "dv"], writes=[K("tT")]); yield
        S.op("dve", I("tensor_tensor", out=tT[:, w_], in0=tT[:, w_], in1=Uk[:, sf], op=ALU.mult), reads=[K("tT"), ("U", st, 1)], writes=[K("tT")]); yield
        S.op("pool", I("tensor_tensor", out=bK[:, w_], in0=tT[:, w_], in1=tC[:, w_], op=ALU.mult), reads=[K("tT"), K("tC")], writes=[K("bK")]); yield
        S.op("act", I("activation", out=bV[:, w_], in_=Uv[:, sf], func=AF.Copy), reads=[("U", st, 2)], writes=[K("bV")]); yield
        S.op("pool", I("tensor_tensor", out=L1st[0:64, w_, 0], in0=tK[0:64, w_], in1=tP[0:64, w_], op=ALU.mult), reads=[K("tK"), K("tP")], writes=[K("L1st")]); yield
        S.op("pool", I("tensor_tensor", out=L1st[64:128, w_, 1], in0=tK[64:128, w_], in1=tP[64:128, w_], op=ALU.mult), reads=[K("tK"), K("tP")], writes=[K("L1st")]); yield
        S.op("dve", I("tensor_tensor", out=L1st[0:64, w_, 2], in0=Ur[0:64, ef], in1=tP[0:64, w_], op=ALU.mult), reads=[("U", st, 0), K("tP")], writes=[K("L1st")]); yield
        S.op("dve", I("tensor_tensor", out=L1st[64:128, w_, 3], in0=Ur[64:128, ef], in1=tP[64:128, w_], op=ALU.mult), reads=[("U", st, 0), K("tP")], writes=[K("L1st")]); yield
        S.op("sp", I("dma_start", out=L1_s.ap()[:, p, e0:e0 + bw, :], in_=L1st[:, w_, :]), reads=[K("L1st")], writes=["L1_s"], dma=f"l1s{k}"); yield
        if own:
            S.op("dve", I("scalar_tensor_tensor", out=tR[:, w_], in0=Ur[:, sf], scalar=vcol(V_RK + p), in1=tT[:, w_], op0=ALU.mult, op1=ALU.mult),
                 reads=[("U", st, 0), K("tT"), "vec"], writes=[K("tR")]); yield
            S.op("pe", I("matmul", P_SS[:, pc], lhsT=blk1, rhs=tR[:, w_], start=True, stop=True), reads=["cmat", K("tR")], writes=["p_ss"]); yield
            S.op("dve", I("tensor_tensor", out=tR[:, w_], in0=P_SS[:, pc], in1=Uv[:, sf], op=ALU.mult), reads=["p_ss", ("U", st, 2)], writes=[K("tR")]); yield
            S.op("act", I("activation", out=tG1[:, w_], in_=P_GL[:, pc], func=AF.Copy, scale=vcol(V_LNW + p)), reads=["p_gl", "vec"], writes=[K("tG1")]); yield
            S.op("dve", I("scalar_tensor_tensor", out=tG2[:, w_], in0=tR[:, w_], scalar=vcol(V_LNB + p), in1=P_GL[:, pc], op0=ALU.add, op1=ALU.mult),
                 reads=[K("tR"), "p_gl", "vec"], writes=[K("tG2")]); yield
            S.op("sp", I("dma_start", out=G_s.ap()[0, :, p, e0:e0 + bw], in_=tG1[:, w_]), reads=[K("tG1")], writes=["G_s"], dma=f"g1s{k}"); yield
            S.op("sp", I("dma_start", out=G_s.ap()[1, :, p, e0:e0 + bw], in_=tG2[:, w_]), reads=[K("tG2")], writes=["G_s"], dma=f"g2s{k}"); yield
        for sb in range(bw // 128):
            cs = slice(sb * 128, sb * 128 + 128)
            eb = e0 + sb * 128
            tb = 384 * (k % 2)
            P_T = P_TRb if k < 2 else P_TRb2
            ptk = "p_tr" if k < 2 else "p_tr2"
            S.op("pe", [I("transpose", P_T[:, tb:tb + 128], bB[:, cs], identb[:]),
                        I("transpose", P_T[:, tb + 128:tb + 256], bK[:, cs], identb[:]),
                        I("transpose", P_T[:, tb + 256:tb + 384], bV[:, cs], identb[:])],
                 reads=[K("bB"), K("bK"), K("bV"), "identb"], writes=[ptk]); yield
            for wh in range(2):
                src = bcast(P_T[:, tb + wh * 128:tb + wh * 128 + 1], [[64, 2], [1, 64]])
                dst = bcast(stBK[:, wh, 0, 0:1], [[192, 2], [1, 64]])
                S.op("act", I("activation", out=dst, in_=src, func=AF.Copy, scale=(-1.0 if wh == 0 else 1.0)), reads=[ptk], writes=[K("stBK")]); yield
            S.op("act", I("activation", out=stV[:], in_=P_T[:, tb + 256:tb + 384], func=AF.Copy), reads=[ptk], writes=[K("stV")]); yield
            S.op("sp", I("dma_start", out=bk_s.ap()[:, :, p, eb:eb + 128, :].rearrange("w r t j -> t w r j"), in_=stBK[:]),
                 reads=[K("stBK")], writes=["bk_s"], dma=f"bks{k}"); yield
            S.op("sp", I("dma_start", out=v_s.ap()[:, p, eb:eb + 128, :].rearrange("r t i -> t r i"), in_=stV[:].rearrange("t (r i) -> t r i", r=2)),
                 reads=[K("stV")], writes=["v_s"], dma=f"vs{k}"); yield

    def run_interleaved(gens):
        active = list(gens)
        while active:
            for g in list(active):
                try:
                    next(g)
                except StopIteration:
                    active.remove(g)

    def project_gen(wt, evac, blocks=(0, 1, 2, 3, 4)):
        issue_wload()
        sl = wt_loaded[wt]
        for bi in blocks:
            c0, n = TBLK[bi]
            b = pj_rot[0] % 2
            pj_rot[0] += 1
            pt = PS[b]
            fns = [I("matmul", pt[:, 0:n], lhsT=wslot[sl][:, kc, :], rhs=xb[:, kc, c0:c0 + n], start=(kc == 0), stop=(kc == KC - 1))
                   for kc in range(KC)]
            S.op("pe", fns, reads=[("win", sl), ("xb", bi)], writes=[("pj", b)])
            evac(bi, pt, b, c0, n)
            yield

    def pair_proj_gen(p, st):
        for ui, wt0, ti0 in ((0, 16, 0), (1, 24, 8), (2, 32, 16)):
            for _ in project_gen(wt0 + p, evac_rwkv(ui, st)):
                yield
            for _ in shift_ops(ui, ti0 + p, st):
                yield

    def chains_gen(p, st):
        groups_ = [SUBS[i0:i0 + NPAR] for i0 in range(0, 16, NPAR)] + [[SUBS[16]], [SUBS[17]]]
        for grp in groups_:
            gens = [block_gen(p, e0, bw, k, st) for k, (e0, bw) in enumerate(grp)]
            active = list(gens)
            while active:
                for g in list(active):
                    try:
                        next(g)
                    except StopIteration:
                        active.remove(g)
                yield

    for _ in pair_proj_gen(0, 0):
        pass
    for p in range(NPAIR):
        st = p % 2
        cg = chains_gen(p, st)
        pg = pair_proj_gen(p + 1, 1 - st) if p + 1 < NPAIR else iter(())
        rounds = 0
        pg_done = False
        for _ in cg:
            rounds += 1
            if not pg_done and rounds % 4 == 0:
                try:
                    next(pg)
                except StopIteration:
                    pg_done = True
        for _ in pg:
            pass

    S.op("sp", I("dma_start", out=shn_o.ap(), in_=shn[:]), reads=["shn"], dma="o_shn")
    S.barrier()
    A.release(m_rwkv)

    convT = A.alloc("convT", [128, 8, NSEQ, 3], F32)
    lruhT = A.alloc("lruhT", [128, 8, NSEQ], F32)
    lruw = A.alloc("lruw", [128, 2, 8, 128], BF16)
    lruh = A.alloc("lruh", [128, 8, 1 + NSEQ], F32)
    lruc = A.alloc("lruc", [128, 8, 1 + NSEQ, 3], F32)
    S.op("sp", I("dma_start", out=convT[:], in_=convT_h.ap()), writes=["convT"], dma="c2")
    S.op("sp", I("dma_start", out=lruhT[:], in_=lruhT_h.ap()), writes=["lruhT"], dma="c2")
    S.op("pool", I("dma_start", out=lruw[:], in_=lruw_h.ap().rearrange("w n c d -> c w n d")), writes=["lruw"], dma="c3")
    Ux = A.alloc("Ux", [128, LW], F32)
    xc = A.alloc("xc", [128, LW], F32)
    xcb = A.alloc("xcb", [128, LW], BF16)
    Rg = A.alloc("Rg", [128, LW], F32)
    Ig = A.alloc("Ig", [128, LW], F32)
    Mg = A.alloc("Mg", [128, LW], F32)
    Hh = A.alloc("Hh", [128, LW], F32)
    Gt = A.alloc("Gt", [128, NOT], F32)
    G2t = A.alloc("G2t", [128, NOT], F32)
    yl = A.alloc("yl", [128, NOT], BF16)
    h0o = A.alloc("h0o", [128, 1], F32)
    S.op("pool", I("memset", Ux[:], 0.0), writes=["Ux"])
    LB = [(3, 512), (515, 512), (1027, 512), (1539, 512), (2051, 176)]
    SP0 = 2051
    LV = 2227

    def evac_x(bi, pt, b, c0, n):
        if bi < 4:
            S.op("act", I("activation", out=Ux[:, 3 + c0:3 + c0 + n], in_=pt[:, 0:n], func=AF.Copy), reads=[("pj", b)], writes=["Ux"])
        else:
            dst = bcast(Ux[:, SP0 + 3:SP0 + 4], [[11, NSEQ], [1, TS]])
            S.op("act", I("activation", out=dst, in_=pt[:, 0:NS].rearrange("p (g s) -> p g s", s=TS), func=AF.Copy),
                 reads=[("pj", b)], writes=["Ux"])

    def evac_g(bi, pt, b, c0, n):
        o0 = c0 - NPRE
        S.op("act", I("activation", out=Gt[:, o0:o0 + n], in_=pt[:, 0:n], func=AF.Copy), reads=[("pj", b)], writes=["Gt"])

    for c in range(8):
        project(c, evac_x)
        S.op("pool", I("tensor_copy", out=bcast(Ux[:, SP0:SP0 + 1], [[11, NSEQ], [1, 3]]), in_=convT[:, c, :, :]),
             reads=["convT", "Ux"], writes=["Ux"])
        S.op("pool", I("tensor_copy", out=lruc[:, c, 0, :], in_=Ux[:, 2048:2051]), reads=["Ux"], writes=["lruc"])
        S.op("pool", I("tensor_copy", out=lruc[:, c, 1:1 + NSEQ, :], in_=bcast(Ux[:, SP0 + 8:SP0 + 9], [[11, NSEQ], [1, 3]])),
             reads=["Ux"], writes=["lruc"])
        W3 = LW - 3
        S.op("dve", I("tensor_scalar", out=xc[:, 3:LV], in0=Ux[:, 3:LV], scalar1=vcol(V_CW + 4 * c + 3), scalar2=vcol(V_CB + c),
                      op0=ALU.mult, op1=ALU.add), reads=["Ux", "vec"], writes=["xc"])
        for j in (1, 2, 3):
            S.op("dve", I("scalar_tensor_tensor", out=xc[:, 3:LV], in0=Ux[:, 3 - j:LV - j], scalar=vcol(V_CW + 4 * c + 3 - j),
                          in1=xc[:, 3:LV], op0=ALU.mult, op1=ALU.add), reads=["Ux", "xc", "vec"], writes=["xc"])
        S.op("act", I("activation", out=xcb[:, 3:LV], in_=xc[:, 3:LV], func=AF.Copy), reads=["xc"], writes=["xcb"])
        for (l0, n) in LB:
            S.op("pe", I("matmul", PS[2][:, 0:n], lhsT=lruw[:, 0, c, :], rhs=xcb[:, l0:l0 + n], start=True, stop=True),
                 reads=["lruw", "xcb"], writes=["p_wl"])
            S.op("act", I("activation", out=Rg[:, l0:l0 + n], in_=PS[2][:, 0:n], func=AF.Sigmoid, bias=vcol(V_BA + c)),
                 reads=["p_wl", "vec"], writes=["Rg"])
            S.op("pe", I("matmul", PS[3][:, 0:n], lhsT=lruw[:, 1, c, :], rhs=xcb[:, l0:l0 + n], start=True, stop=True),
                 reads=["lruw", "xcb"], writes=["p_al"])
            S.op("act", I("activation", out=Ig[:, l0:l0 + n], in_=PS[3][:, 0:n], func=AF.Sigmoid, bias=vcol(V_BX + c)),
                 reads=["p_al", "vec"], writes=["Ig"])
        S.op("act", I("activation", out=Mg[:, 3:LV], in_=Rg[:, 3:LV], func=AF.Exp, scale=dvcol(DV_CL2 + c)), reads=["Rg", "dv"], writes=["Mg"])
        S.op("act", I("activation", out=Rg[:, 3:LV], in_=Rg[:, 3:LV], func=AF.Exp, scale=dvcol(DV_CL + c)), reads=["Rg", "dv"], writes=["Rg"])
        S.op("pool", I("tensor_scalar", out=Mg[:, 3:LV], in0=Mg[:, 3:LV], scalar1=-1.0, scalar2=1.0, op0=ALU.mult, op1=ALU.add),
             reads=["Mg"], writes=["Mg"])
        S.op("act", I("activation", out=Mg[:, 3:LV], in_=Mg[:, 3:LV], func=AF.Sqrt), reads=["Mg"], writes=["Mg"])
        S.op("pool", I("tensor_tensor", out=Ig[:, 3:LV], in0=Ig[:, 3:LV], in1=xc[:, 3:LV], op=ALU.mult), reads=["Ig", "xc"], writes=["Ig"])
        S.op("dve", I("tensor_tensor", out=Ig[:, 3:LV], in0=Ig[:, 3:LV], in1=Mg[:, 3:LV], op=ALU.mult), reads=["Ig", "Mg"], writes=["Ig"])
        S.op("pool", I("memset", bcast(Rg[:, SP0:SP0 + 1], [[11, NSEQ], [1, 3]]), 0.0), reads=["Rg"], writes=["Rg"])
        S.op("pool", I("memset", bcast(Ig[:, SP0:SP0 + 1], [[11, NSEQ], [1, 3]]), 0.0), reads=["Ig"], writes=["Ig"])
        S.op("pool", I("tensor_copy", out=bcast(Ig[:, SP0 + 2:SP0 + 3], [[11, NSEQ]]), in_=lruhT[:, c, :]), reads=["Ig", "lruhT"], writes=["Ig"])
        S.op("dve", I("tensor_tensor_scan", out=Hh[:, 3:1027], data0=Rg[:, 3:1027], data1=Ig[:, 3:1027], initial=0.0,
                      op0=ALU.mult, op1=ALU.add), reads=["Rg", "Ig"], writes=["Hh"])
        S.op("dve", I("tensor_tensor", out=h0o[:], in0=Hh[:, 1026:1027], in1=vcol(V_FLAG), op=ALU.mult), reads=["Hh", "vec"], writes=["h0o"])
        S.op("dve", I("tensor_tensor_scan", out=Hh[:, 1027:2051], data0=Rg[:, 1027:2051], data1=Ig[:, 1027:2051], initial=h0o[:, 0:1],
                      op0=ALU.mult, op1=ALU.add), reads=["Rg", "Ig", "h0o", "Hh"], writes=["Hh"])
        S.op("dve", I("tensor_tensor_scan", out=Hh[:, SP0:SP0 + 176], data0=Rg[:, SP0:SP0 + 176], data1=Ig[:, SP0:SP0 + 176], initial=0.0,
                      op0=ALU.mult, op1=ALU.add), reads=["Rg", "Ig", "Hh"], writes=["Hh"])
        S.op("pool", I("tensor_copy", out=lruh[:, c, 0:1], in_=Hh[:, 2050:2051]), reads=["Hh"], writes=["lruh"])
        S.op("pool", I("tensor_copy", out=lruh[:, c, 1:1 + NSEQ], in_=bcast(Hh[:, SP0 + 10:SP0 + 11], [[11, NSEQ]])), reads=["Hh"], writes=["lruh"])
        project(8 + c, evac_g, blocks=(2, 3, 4))
        S.op("act", I("activation", out=G2t[:], in_=Gt[:], func=AF.Square), reads=["Gt"], writes=["G2t"])
        S.op("dve", I("tensor_scalar", out=G2t[:], in0=G2t[:], scalar1=0.044715, scalar2=1.0, op0=ALU.mult, op1=ALU.add),
             reads=["G2t"], writes=["G2t"])
        S.op("dve", I("tensor_tensor", out=G2t[:], in0=G2t[:], in1=Gt[:], op=ALU.mult), reads=["G2t", "Gt"], writes=["G2t"])
        S.op("act", I("activation", out=G2t[:], in_=G2t[:], func=AF.Tanh, scale=0.7978845608028654), reads=["G2t"], writes=["G2t"])
        S.op("dve", I("scalar_tensor_tensor", out=G2t[:], in0=G2t[:], scalar=1.0, in1=Gt[:], op0=ALU.add, op1=ALU.mult),
             reads=["G2t", "Gt"], writes=["G2t"])
        S.op("dve", I("scalar_tensor_tensor", out=yl[:, 0:NOWN], in0=G2t[:, 0:NOWN], scalar=0.5, in1=Hh[:, 1027:2051],
                      op0=ALU.mult, op1=ALU.mult), reads=["G2t", "Hh"], writes=["yl"])
        S.op("dve", I("scalar_tensor_tensor", out=yl[:, NOWN:NOT].rearrange("p (g s) -> p g s", s=TS),
                      in0=G2t[:, NOWN:NOT].rearrange("p (g s) -> p g s", s=TS), scalar=0.5,
                      in1=bcast(Hh[:, SP0 + 3:SP0 + 4], [[11, NSEQ], [1, TS]]), op0=ALU.mult, op1=ALU.mult),
             reads=["G2t", "Hh"], writes=["yl"])
        S.op("sp", I("dma_start", out=yT_s.ap()[c], in_=yl[:]), reads=["yl"], writes=[("yT_s", c)], dma="yts")
    S.op("sp", I("dma_start", out=lruh_o.ap(), in_=lruh[:]), reads=["lruh"], dma="o_lh")
    S.op("sp", I("dma_start", out=lruc_o.ap(), in_=lruc[:]), reads=["lruc"], dma="o_lc")
    S.barrier()
    A.release(m_phase1)
    if stop_after == 1:
        S.final_wait("sp")
        S.emit()
        return nc, S, A
    sS_h = din("sS", [NSEQ, 128, NPAIR, 64])
    SP_o = dout("S_p", [128, NPAIR, 64])
    SS_o = dout("S_s", [NSEQ, 128, NPAIR, 64])
    C.o_s = o_s = dscr("o_s", [2, NPAIR, EW, 64], BF16)

    m_phase2 = A.mark()
    SC = 256
    L1b = [A.alloc(f"L1b{i}", [128, NPAIR, SC + 1, 4], BF16) for i in range(2)]
    Gb = [A.alloc(f"Gb{i}", [128, NPAIR, SC + 1], F32) for i in range(2)]
    LT2 = [A.alloc(f"LT2_{i}", [34, NPAIR, 9, 128], BF16) for i in range(2)]
    R2 = [A.alloc(f"R2_{i}", [34, NPAIR, 9, 64], BF16) for i in range(2)]
    Hin = [A.alloc(f"Hin{i}", [128, NPAIR, 64], F32) for i in range(2)]
    Hr = [A.alloc(f"Hr{i}", [128, NPAIR, 64], F32) for i in range(2)]
    Hz = A.alloc("Hz", [128, NPAIR, 64], F32)
    Hbf = A.alloc("Hbf", [128, NPAIR, 64], BF16)
    S.op("pool", I("memset", Hz[:], 0.0), writes=["Hz"])
    for i in range(2):
        S.op("pool", I("memset", LT2[i][:], 0.0), writes=[("LT2b", i), ("LT2k", i)])
        S.op("pool", I("memset", R2[i][:], 0.0), writes=[("R2u", i, g_) for g_ in range(4)] + [("R2v", i)])

    def load_super(sci):
        sc0 = sci * SC
        b = sci % 2
        lo = max(sc0 - 1, 0)
        col0 = lo - (sc0 - 1)
        n = sc0 + SC - lo
        S.op("sp", I("dma_start", out=L1b[b][:, :, col0:col0 + n, :], in_=L1_s.ap()[:, :, lo:lo + n, :]),
             reads=["L1_s"], writes=[("L1b", b)], dma=f"l1b{b}")
        S.op("sp", I("dma_start", out=Gb[b][:, :, col0:col0 + n], in_=w_s.ap()[:, :, lo:lo + n]),
             reads=["w_s"], writes=[("Gb", b)], dma=f"gb{b}")

    groups = []
    for q in range(256):
        groups.append((8 * q, 8, q == 255, "p", q))
    for g in range(NSEQ):
        groups.append((SB0 + 9 * g, 8, True, "s", g))

    if C.debug and getattr(build, "only_groups", None) is not None:
        groups = [groups[i] for i in build.only_groups]
    C.dbg_tiles = dict(R2=R2, LT2=LT2, Hbf=Hbf, Hin=Hin, Hr=Hr, L1b=L1b, Gb=Gb)

    def load_rows(gi):
        e0, nst, fl, kind, idx = groups[gi]
        sl = gi % 2
        S.op("sp", I("dma_start", out=LT2[sl][0:2, :, 0:nst, :], in_=bk_s.ap()[0, :, :, e0:e0 + nst, :]),
             reads=["bk_s"], writes=[("LT2b", sl)], dma=f"rb{sl}")
        S.op("sp", I("dma_start", out=LT2[sl][32:34, :, 0:nst, :], in_=bk_s.ap()[1, :, :, e0:e0 + nst, :]),
             reads=["bk_s"], writes=[("LT2k", sl)], dma=f"rk{sl}")
        S.op("sp", I("dma_start", out=R2[sl][32:34, :, 0:nst, :], in_=v_s.ap()[:, :, e0:e0 + nst, :]),
             reads=["v_s"], writes=[("R2v", sl)], dma=f"rv{sl}")

    NG = 4
    GP = NPAIR // NG
    P_Hg = [PS[0], PS[1], PS[2], PS[3]]
    P_Ug = [PS[4], PS[5], PS[6], PS[7]]

    def gsl(g):
        return slice(GP * g, GP * g + GP)

    def set_state(g, src_tile, src_key):
        fns = [I("matmul", P_Hg[g][:, 64 * q:64 * q + 64], lhsT=ident, rhs=src_tile[:, GP * g + q, :], start=(q == 0), stop=True,
                 skip_group_check=True) for q in range(GP)]
        S.op("pe", fns, reads=["cmat", src_key], writes=[("psH", g)])
        S.op("dve", I("tensor_copy", out=Hbf[:, gsl(g), :], in_=src_tile[:, gsl(g), :]), reads=[src_key], writes=[("Hbf", g)])

    hr_rot = [0] * NG

    def renorm(g, e):
        b = (e // SC) % 2
        col = (e - 1) - ((e // SC) * SC - 1)
        i = hr_rot[g] % 2
        hr_rot[g] += 1
        g_ap = bcast(Gb[b][:, GP * g, col:col + 1], [[SC + 1, GP], [0, 64]])
        S.op("dve", I("tensor_tensor", out=Hr[i][:, gsl(g), :], in0=P_Hg[g][:, 0:64 * GP].rearrange("p (a b) -> p a b", b=64), in1=g_ap, op=ALU.mult),
             reads=[("psH", g), ("Gb", b)], writes=[("Hr", i, g)])
        return Hr[i], ("Hr", i, g)

    load_super(0)
    load_rows(0)
    next_super = 1
    for gi, (e0, nst, fl, kind, idx) in enumerate(groups):
        sl = gi % 2
        if gi + 1 < len(groups):
            load_rows(gi + 1)
        last_e = e0 + nst
        while next_super <= (e0 // SC) + 1 and next_super * SC < EW - SC + 1:
            load_super(next_super)
            next_super += 1
        nxt = groups[gi + 1] if gi + 1 < len(groups) else None
        if nxt is not None and nxt[3] == "s":
            hi = nxt[4] % 2
            S.op("sp", I("dma_start", out=Hin[hi][:], in_=sS_h.ap()[nxt[4]]), writes=[("Hin", hi, g_) for g_ in range(NG)], dma=f"hin{hi}")
        nent = nst + (1 if fl else 0)
        for s in range(nent):
            e = e0 + s
            b = (e // SC) % 2
            col = e - ((e // SC) * SC - 1)
            is_flush = s == nst
            seq_start = (s == 0) and (kind == "s" or e == 0)
            for g in range(NG):
                if seq_start:
                    if kind == "p":
                        set_state(g, Hz, "Hz")
                    else:
                        set_state(g, Hin[idx % 2], ("Hin", idx % 2, g))
                elif is_flush or (kind == "p" and e % CH == 0):
                    ht, hk = renorm(g, e)
                    if is_flush:
                        dst = SP_o.ap() if kind == "p" else SS_o.ap()[idx]
                        S.op("sp", I("dma_start", out=dst[:, gsl(g), :], in_=ht[:, gsl(g), :]), reads=[hk], dma=f"so{g}")
                    set_state(g, ht, hk)
            for g in range(NG):
                fns = [I("matmul", P_Ug[g][0:4, 64 * q:64 * q + 64], lhsT=L1b[b][:, GP * g + q, col, :], rhs=Hbf[:, GP * g + q, :], start=True, stop=True)
                       for q in range(GP)]
                S.op("pe", fns, reads=[("L1b", b), ("Hbf", g)], writes=[("psU", g)])
            for g in range(NG):
                S.op("act", I("activation", out=R2[sl][0:4, gsl(g), s, :], in_=P_Ug[g][0:4, 0:64 * GP].rearrange("p (a b) -> p a b", b=64), func=AF.Copy),
                     reads=[("psU", g)], writes=[("R2u", sl, g)])
            if not is_flush:
                for g in range(NG):
                    fns = [I("matmul", P_Hg[g][:, 64 * q:64 * q + 64], lhsT=LT2[sl][0:34, GP * g + q, s, :], rhs=R2[sl][0:34, GP * g + q, s, :],
                             start=False, stop=True, skip_group_check=True) for q in range(GP)]
                    S.op("pe", fns, reads=[("LT2b", sl), ("LT2k", sl), ("R2u", sl, g), ("R2v", sl)], writes=[("psH", g)])
                for g in range(NG):
                    S.op("dve", I("tensor_copy", out=Hbf[:, gsl(g), :], in_=P_Hg[g][:, 0:64 * GP].rearrange("p (a b) -> p a b", b=64)),
                         reads=[("psH", g)], writes=[("Hbf", g)])
        S.op("sp", I("dma_start", out=o_s.ap()[:, :, e0:e0 + nent, :], in_=R2[sl][2:4, :, 0:nent, :]),
             reads=[("R2u", sl, g_) for g_ in range(NG)], writes=["o_s"], dma=f"os{sl}")
    S.barrier()
    A.release(m_phase2)
    if stop_after == 2:
        S.final_wait("sp")
        S.emit()
        return nc, S, A
    memT_h = din("memT", [D, 256])
    wk_h = din("wk_t", [16, 128, KC, 128])
    wv_h = din("wv_r", [4, 128, KC, 512])
    wout_h = din("wout_t", [16, 128, KC, 128])
    wq_h = din("wq_t", [16, 128, KC, 128])
    wo_h = din("wo_t", [16, 128, KC, 128])
    w1_h = din("w1_t", [64, 128, KC, 128])
    w2_h = din("w2_t", [16, 4, 128, KC, 128])
    cKT_h = din("cKT", [NSEQ, 128, 16, 256])
    cV_h = din("cV", [NSEQ, 256, D])
    yT_o = dout("yT", [D, NOT])
    mkT_o = dout("mkT", [D, 256])
    mv_o = dout("mv", [256, D])

    onesb = A.alloc("onesb", [128, 128], BF16)
    S.op("dve", I("tensor_copy", out=onesb[:], in_=ones), reads=["cmat"], writes=["onesb"])
    KTp = A.alloc("KTp", [128, 16, 256], BF16)
    Vp = A.alloc("Vp", [128, 2, D], BF16)
    ATT_SCALE = 512.0 ** -0.5

    class WStream:
        def __init__(self, name, nslots, shape):
            self.name, self.n = name, nslots
            self.slots = [A.alloc(f"{name}{i}", shape, BF16) for i in range(nslots)]
            self.queue, self.issued, self.used = [], 0, 0

        def extend(self, aps):
            self.queue.extend(aps)

        def _issue(self):
            if self.issued < len(self.queue):
                sl = self.issued % self.n
                S.op("pool", I("dma_start", out=self.slots[sl][:], in_=self.queue[self.issued]),
                     writes=[(self.name, sl)], dma=f"{self.name}{sl}")
                self.issued += 1

        def get(self, k=1):
            assert k <= self.n
            while self.issued < min(len(self.queue), self.used + self.n):
                self._issue()
            out = []
            for _ in range(k):
                sl = self.used % self.n
                self.used += 1
                out.append((self.slots[sl], (self.name, sl)))
            return out

        def next(self):
            return self.get(1)[0]

    m3a = A.mark()
    memb = A.alloc("memb", [128, KC, 256], BF16)
    S.op("pool", I("dma_start", out=memb[:], in_=memT_h.ap().rearrange("(kc kp) m -> kp kc m", kp=128)), writes=["memb"], dma="c3")
    ws_a = WStream("wa", 4, [128, KC, 128])
    ws_a.extend([wk_h.ap()[i] for i in range(16)])
    kst = [A.alloc(f"kst{i}", [128, 256], F32) for i in range(2)]
    for ft in range(16):
        wt, wk_ = ws_a.next()
        pt = PS[ft % 2]
        S.op("pe", [I("matmul", pt[:, 0:256], lhsT=wt[:, kc, :], rhs=memb[:, kc, :], start=(kc == 0), stop=(kc == KC - 1)) for kc in range(KC)],
             reads=[wk_, "memb"], writes=[("ps", ft % 2)])
        S.op("act", I("activation", out=KTp[:, ft, :], in_=pt[:, 0:256], func=AF.Copy), reads=[("ps", ft % 2)], writes=["KTp"])
        S.op("dve", I("tensor_copy", out=kst[ft % 2][:], in_=pt[:, 0:256]), reads=[("ps", ft % 2)], writes=[("kst", ft % 2)])
        S.op("sp", I("dma_start", out=mkT_o.ap()[128 * ft:128 * ft + 128, :], in_=kst[ft % 2][:]), reads=[("kst", ft % 2)], dma=f"mk{ft % 2}")
    wvs = [A.alloc(f"wvs{i}", [128, KC, 512], BF16) for i in range(2)]
    vst = [A.alloc(f"vst{i}", [128, 512], F32) for i in range(2)]
    for fb in range(4):
        S.op("pool", I("dma_start", out=wvs[fb % 2][:], in_=wv_h.ap()[fb]), writes=[("wvs", fb % 2)], dma=f"wvs{fb % 2}")
        for mc in range(2):
            j = fb * 2 + mc
            pt = PS[2 + j % 2]
            S.op("pe", [I("matmul", pt[:, 0:512], lhsT=memb[:, kc, 128 * mc:128 * mc + 128], rhs=wvs[fb % 2][:, kc, :],
                          start=(kc == 0), stop=(kc == KC - 1)) for kc in range(KC)],
                 reads=[("wvs", fb % 2), "memb"], writes=[("ps", 2 + j % 2)])
            S.op("act", I("activation", out=Vp[:, mc, 512 * fb:512 * fb + 512], in_=pt[:, 0:512], func=AF.Copy), reads=[("ps", 2 + j % 2)], writes=["Vp"])
            S.op("dve", I("tensor_copy", out=vst[j % 2][:], in_=pt[:, 0:512]), reads=[("ps", 2 + j % 2)], writes=[("vst", j % 2)])
            S.op("sp", I("dma_start", out=mv_o.ap()[128 * mc:128 * mc + 128, 512 * fb:512 * fb + 512], in_=vst[j % 2][:]),
                 reads=[("vst", j % 2)], dma=f"mv{j % 2}")
    S.barrier()
    A.release(m3a)
    if stop_after == 2.5:
        S.final_wait("sp")
        S.emit()
        return nc, S, A

    m3b = A.mark()
    yrw = A.alloc("yrw", [128, NPAIR, NOT], BF16)
    Ot = [A.alloc(f"Ot{i}", [128, NPAIR, 2, 64], BF16) for i in range(2)]
    Of = A.alloc("Of", [128, 16, 64], F32)
    Osq = A.alloc("Osq", [128, 16, 64], F32)
    st1 = A.alloc("st1", [128, 16], F32)
    st2 = A.alloc("st2", [128, 16], F32)
    st3 = A.alloc("st3", [128, 16], F32)
    Gt1 = [A.alloc(f"Gt1_{i}", [128, NPAIR, 144], F32) for i in range(2)]
    Gt2 = [A.alloc(f"Gt2_{i}", [128, NPAIR, 144], F32) for i in range(2)]
    ytmp = A.alloc("ytmp", [128, 128], F32)
    def load_tile_3b(tt):
        sl = tt % 2
        if tt < 8:
            ec0 = 1025 + 128 * tt
            for hh in range(2):
                S.op("sp", I("dma_start", out=Ot[sl][:, :, hh, :], in_=o_s.ap()[hh, :, ec0:ec0 + 128, :].rearrange("p e i -> e p i")),
                     reads=["o_s"], writes=[("Ot", sl)], dma=f"ot{sl}")
            S.op("sp", I("dma_start", out=Gt1[sl][:, :, 0:128], in_=G_s.ap()[0, :, :, ec0 - 1:ec0 + 127]), reads=["G_s"], writes=[("Gt1", sl)], dma=f"gt1{sl}")
            S.op("sp", I("dma_start", out=Gt2[sl][:, :, 0:128], in_=G_s.ap()[1, :, :, ec0 - 1:ec0 + 127]), reads=["G_s"], writes=[("Gt2", sl)], dma=f"gt2{sl}")
        else:
            for g in range(NSEQ):
                ecg = SB0 + 9 * g + 1
                for hh in range(2):
                    S.op("sp", I("dma_start", out=Ot[sl][8 * g:8 * g + 8, :, hh, :], in_=o_s.ap()[hh, :, ecg:ecg + 8, :].rearrange("p e i -> e p i")),
                         reads=["o_s"], writes=[("Ot", sl)], dma=f"ot{sl}")
            S.op("sp", I("dma_start", out=Gt1[sl][:, :, 0:144], in_=G_s.ap()[0, :, :, SB0:SB0 + 144]), reads=["G_s"], writes=[("Gt1", sl)], dma=f"gt1{sl}")
            S.op("sp", I("dma_start", out=Gt2[sl][:, :, 0:144], in_=G_s.ap()[1, :, :, SB0:SB0 + 144]), reads=["G_s"], writes=[("Gt2", sl)], dma=f"gt2{sl}")

    load_tile_3b(0)
    for tt in range(9):
        sl = tt % 2
        if tt + 1 < 9:
            load_tile_3b(tt + 1)
        Ofl = Of[:].rearrange("t h i -> t (h i)")
        S.op("act", I("activation", out=Ofl, in_=Ot[sl][:].rearrange("t p h i -> t (p h i)"), func=AF.Copy), reads=[("Ot", sl)], writes=["Of"])
        S.op("dve", I("tensor_reduce", out=st1[:], in_=Of[:], axis=mybir.AxisListType.X, op=ALU.add), reads=["Of"], writes=["st1"])
        S.op("act", I("activation", out=Osq[:].rearrange("t h i -> t (h i)"), in_=Ofl, func=AF.Square), reads=["Of"], writes=["Osq"])
        S.op("dve", I("tensor_reduce", out=st2[:], in_=Osq[:], axis=mybir.AxisListType.X, op=ALU.add), reads=["Osq"], writes=["st2"])
        S.op("dve", I("tensor_scalar", out=st1[:], in0=st1[:], scalar1=1.0 / 64, scalar2=None, op0=ALU.mult), reads=["st1"], writes=["st1"])
        S.op("dve", I("tensor_tensor", out=st3[:], in0=st1[:], in1=st1[:], op=ALU.mult), reads=["st1"], writes=["st3"])
        S.op("dve", I("scalar_tensor_tensor", out=st2[:], in0=st2[:], scalar=1.0 / 64, in1=st3[:], op0=ALU.mult, op1=ALU.subtract),
             reads=["st2", "st3"], writes=["st2"])
        S.op("dve", I("tensor_scalar", out=st2[:], in0=st2[:], scalar1=GN_EPS, scalar2=None, op0=ALU.add), reads=["st2"], writes=["st2"])
        S.op("act", I("activation", out=st2[:], in_=st2[:], func=AF.Sqrt), reads=["st2"], writes=["st2"])
        S.op("dve", I("reciprocal", out=st2[:], in_=st2[:]), reads=["st2"], writes=["st2"])
        S.op("dve", I("tensor_tensor", out=Of[:], in0=Of[:], in1=bcast(st1[:, 0:1], [[1, 16], [0, 64]]), op=ALU.subtract), reads=["Of", "st1"], writes=["Of"])
        S.op("dve", I("tensor_tensor", out=Of[:], in0=Of[:], in1=bcast(st2[:, 0:1], [[1, 16], [0, 64]]), op=ALU.mult), reads=["Of", "st2"], writes=["Of"])
        for half4 in range(2):
            pb = PS[4 + half4]
            S.op("pe", [I("transpose", pb[:, 128 * q:128 * q + 128], Of[:, 2 * (4 * half4 + q):2 * (4 * half4 + q) + 2, :].rearrange("t h i -> t (h i)"), ident)
                        for q in range(4)], reads=["Of", "cmat"], writes=[("ps", 4 + half4)])
            for q in range(4):
                p = 4 * half4 + q
                if tt < 8:
                    g1 = Gt1[sl][:, p, 0:128]
                    g2 = Gt2[sl][:, p, 0:128]
                    dst = yrw[:, p, 128 * tt:128 * tt + 128]
                    src = pb[:, 128 * q:128 * q + 128]
                    tmp = ytmp[:]
                else:
                    g1 = bcast(Gt1[sl][:, p, 0:1], [[9, NSEQ], [1, TS]])
                    g2 = bcast(Gt2[sl][:, p, 0:1], [[9, NSEQ], [1, TS]])
                    dst = yrw[:, p, NOWN:NOT].rearrange("c (g s) -> c g s", s=TS)
                    src = pb[:, 128 * q:128 * q + 128].rearrange("c (g s) -> c g s", s=TS)
                    tmp = ytmp[:].rearrange("c (g s) -> c g s", s=TS)
                S.op("dve", I("tensor_tensor", out=tmp, in0=src, in1=g1, op=ALU.mult), reads=[("ps", 4 + half4), ("Gt1", sl)], writes=["ytmp"])
                S.op("dve", I("tensor_tensor", out=dst, in0=tmp, in1=g2, op=ALU.add), reads=["ytmp", ("Gt2", sl)], writes=["yrw"])
    for p in range(NPAIR):
        S.op("sp", I("dma_start", out=yT_s.ap()[8 + p], in_=yrw[:, p, :]), reads=["yrw"], writes=[("yT_s", 8 + p)], dma="yts")
    S.barrier()
    A.release(m3b)
    if stop_after == 3:
        S.final_wait("sp")
        S.emit()
        return nc, S, A

    TB = 576
    SPL = [(0, 288), (288, 288)]
    XA = A.alloc("XA", [128, 16, TB], F32)
    XB = A.alloc("XB", [128, 16, TB], BF16)
    mean_t = A.alloc("mean_t", [128, TB], F32)
    rstd_t = A.alloc("rstd_t", [128, TB], F32)
    lnt = [A.alloc(f"lnt{i}", [128, TB], F32) for i in range(2)]
    r1b = [A.alloc(f"r1b{i}", [128, TB], BF16) for i in range(2)]
    sqb = [A.alloc(f"sqb{i}", [128, TB], BF16) for i in range(2)]
    ws = WStream("w16", 8, [128, KC, 128])
    for blk in range(2):
        ws.extend([wout_h.ap()[i] for i in range(16)] + [wq_h.ap()[i] for i in range(16)] + [wo_h.ap()[i] for i in range(16)]
                  + [w1_h.ap()[i] for i in range(64)] + [w2_h.ap()[i, qq] for i in range(16) for qq in range(4)])
    lin_rot = [0]
    P_S1 = [PS[4], PS[5]]
    P_S2 = [PS[6], PS[7]]

    def linear_resid_ln(tag, stream, nkc, rhs_fn, rhs_keys, resid_fn, g_col, b_col, out_final=None, kpt=KC, rhs_key_fn=None):
        def stats(ft):
            j = ft % 2
            for si, (s0, sn) in enumerate(SPL):
                S.op("pe", I("matmul", P_S1[si][:, 0:sn], lhsT=onesb[:], rhs=r1b[j][:, s0:s0 + sn], start=(ft == 0), stop=(ft == 15)),
                     reads=["onesb", ("r1b", j)], writes=[("ps", 4 + si)])
                S.op("pe", I("matmul", P_S2[si][:, 0:sn], lhsT=onesb[:], rhs=sqb[j][:, s0:s0 + sn], start=(ft == 0), stop=(ft == 15)),
                     reads=["onesb", ("sqb", j)], writes=[("ps", 6 + si)])

        for ft in range(16):
            wts = stream.get(nkc // kpt)
            res_ap, res_keys = resid_fn(ft)
            for si, (s0, sn) in enumerate(SPL):
                b = lin_rot[0] % 3
                lin_rot[0] += 1
                pt = PS[b]
                mms = [I("matmul", pt[:, 0:sn], lhsT=wts[kc // kpt][0][:, kc % kpt, :], rhs=rhs_fn(kc, s0, sn), start=(kc == 0), stop=(kc == nkc - 1))
                       for kc in range(nkc)]
                if ft == 0 and rhs_key_fn is not None:
                    for kc in range(nkc):
                        S.op("pe", mms[kc], reads=[w_[1] for w_ in wts] + [rhs_key_fn(kc)], writes=[("ps", b)])
                else:
                    S.op("pe", mms, reads=[w_[1] for w_ in wts] + rhs_keys, writes=[("ps", b)])
                S.op("dve", I("scalar_tensor_tensor", out=XA[:, ft, s0:s0 + sn], in0=res_ap[:, s0:s0 + sn], scalar=ALPHA, in1=pt[:, 0:sn],
                              op0=ALU.mult, op1=ALU.add), reads=[("ps", b)] + res_keys, writes=[("XA", ft)])
            j = ft % 2
            S.op("act", I("activation", out=r1b[j][:], in_=XA[:, ft, :], func=AF.Copy), reads=[("XA", ft)], writes=[("r1b", j)])
            S.op("act", I("activation", out=sqb[j][:], in_=XA[:, ft, :], func=AF.Square), reads=[("XA", ft)], writes=[("sqb", j)])
            if ft > 0:
                stats(ft - 1)
        stats(15)
        for si, (s0, sn) in enumerate(SPL):
            S.op("act", I("activation", out=mean_t[:, s0:s0 + sn], in_=P_S1[si][:, 0:sn], func=AF.Copy, scale=1.0 / D), reads=[("ps", 4 + si)], writes=["mean_t"])
            S.op("act", I("activation", out=rstd_t[:, s0:s0 + sn], in_=P_S2[si][:, 0:sn], func=AF.Copy, scale=1.0 / D), reads=[("ps", 6 + si)], writes=["rstd_t"])
        S.op("dve", I("tensor_tensor", out=lnt[0][:], in0=mean_t[:], in1=mean_t[:], op=ALU.mult), reads=["mean_t"], writes=[("lnt", 0)])
        S.op("dve", I("tensor_tensor", out=rstd_t[:], in0=rstd_t[:], in1=lnt[0][:], op=ALU.subtract), reads=["rstd_t", ("lnt", 0)], writes=["rstd_t"])
        S.op("dve", I("tensor_scalar", out=rstd_t[:], in0=rstd_t[:], scalar1=LN_EPS, scalar2=None, op0=ALU.add), reads=["rstd_t"], writes=["rstd_t"])
        S.op("act", I("activation", out=rstd_t[:], in_=rstd_t[:], func=AF.Sqrt), reads=["rstd_t"], writes=["rstd_t"])
        S.op("dve", I("reciprocal", out=rstd_t[:], in_=rstd_t[:]), reads=["rstd_t"], writes=["rstd_t"])
        for ft in range(16):
            j = ft % 2
            S.op("dve", I("tensor_tensor", out=lnt[j][:], in0=XA[:, ft, :], in1=mean_t[:], op=ALU.subtract), reads=[("XA", ft), "mean_t"], writes=[("lnt", j)])
            S.op("dve", I("tensor_tensor", out=lnt[j][:], in0=lnt[j][:], in1=rstd_t[:], op=ALU.mult), reads=[("lnt", j), "rstd_t"], writes=[("lnt", j)])
            S.op("act", I("activation", out=XA[:, ft, :], in_=lnt[j][:], func=AF.Identity, scale=vcol(g_col + ft), bias=vcol(b_col + ft)),
                 reads=[("lnt", j), "vec"], writes=[("XA", ft)])
            if out_final is not None:
                out_final(ft)
            else:
                S.op("act", I("activation", out=XB[:, ft, :], in_=XA[:, ft, :], func=AF.Copy), reads=[("XA", ft)], writes=[("XB", ft)])

    if debug:
        dbgT = {nm: dout("dbg_" + nm, [128, 16, NOT], dt) for nm, dt in (("x1", F32), ("q", BF16), ("att", BF16), ("x2", F32))}

    def dump(nm, src, keys, t0):
        if debug:
            S.op("sp", I("dma_start", out=dbgT[nm].ap()[:, :, t0:t0 + TB], in_=src[:]), reads=keys, dma="dbg")

    for blk in range(2):
        t0 = blk * TB
        mA = A.mark()
        yT = A.alloc("yT", [128, 16, TB], BF16)
        S.op("sp", I("dma_start", out=yT[:], in_=yT_s.ap()[:, :, t0:t0 + TB].rearrange("c p t -> p c t")),
             reads=[("yT_s", c) for c in range(16)], writes=["yT"], dma="c0")
        xr = [A.alloc(f"xr{i}", [128, TB], F32) for i in range(2)]

        def resid_x(ft):
            j = ft % 2
            S.op("sp", I("dma_start", out=xr[j][:], in_=xT_h.ap()[128 * ft:128 * ft + 128, NPRE + t0:NPRE + t0 + TB]), writes=[("xr", j)], dma=f"xr{j}")
            return xr[j], [("xr", j)]

        linear_resid_ln("wout", ws, KC, lambda kc, s0, sn: yT[:, kc, s0:s0 + sn], ["yT"], resid_x, V_LN1G, V_LN1B)
        dump("x1", XA, [("XA", k) for k in range(16)], t0)
        S.barrier()
        A.release(mA)
        mB = A.mark()
        qT = A.alloc("qT", [128, 16, TB], BF16)
        attT = A.alloc("attT", [128, 16, TB], BF16)
        for ft in range(16):
            wt, wkey = ws.next()
            for si, (s0, sn) in enumerate(SPL):
                b = lin_rot[0] % 3
                lin_rot[0] += 1
                pt = PS[b]
                mms = [I("matmul", pt[:, 0:sn], lhsT=wt[:, kc, :], rhs=XB[:, kc, s0:s0 + sn], start=(kc == 0), stop=(kc == KC - 1))
                       for kc in range(KC)]
                if ft == 0:
                    for kc in range(KC):
                        S.op("pe", mms[kc], reads=[wkey, ("XB", kc)], writes=[("ps", b)])
                else:
                    S.op("pe", mms, reads=[wkey] + [("XB", kc) for kc in range(16)], writes=[("ps", b)])
                S.op("act", I("activation", out=qT[:, ft, s0:s0 + sn], in_=pt[:, 0:sn], func=AF.Copy), reads=[("ps", b)], writes=[("qT", ft)])
        pcols = [(0, 288), (288, 288)] if blk == 0 else [(0, 224), (224, 224)]
        PT = [A.alloc(f"PT{i}", [128, 2, 288], BF16) for i in range(2)]
        rden = [A.alloc(f"rden{i}", [128, 288], F32) for i in range(2)]
        it = 0
        for h in range(4):
            for (c0, cn) in pcols:
                j = it % 2
                it += 1
                for mc in range(2):
                    pt = PS[3 + mc]
                    S.op("pe", [I("matmul", pt[:, 0:cn], lhsT=KTp[:, 4 * h + dc, 128 * mc:128 * mc + 128], rhs=qT[:, 4 * h + dc, c0:c0 + cn],
                                  start=(dc == 0), stop=(dc == 3)) for dc in range(4)],
                         reads=["KTp"] + [("qT", 4 * h + dc) for dc in range(4)], writes=[("ps", 3 + mc)])
                    S.op("act", I("activation", out=PT[j][:, mc, 0:cn], in_=pt[:, 0:cn], func=AF.Exp, scale=ATT_SCALE), reads=[("ps", 3 + mc)], writes=[("PT", j)])
                S.op("pe", [I("matmul", PS[5][:, 0:cn], lhsT=onesb[:], rhs=PT[j][:, mc, 0:cn], start=(mc == 0), stop=(mc == 1)) for mc in range(2)],
                     reads=["onesb", ("PT", j)], writes=[("ps", 5)])
                S.op("dve", I("reciprocal", out=rden[j][:, 0:cn], in_=PS[5][:, 0:cn]), reads=[("ps", 5)], writes=[("rden", j)])
                for dc in range(4):
                    pt = PS[6 + dc % 2]
                    S.op("pe", [I("matmul", pt[:, 0:cn], lhsT=Vp[:, mc, 128 * (4 * h + dc):128 * (4 * h + dc) + 128], rhs=PT[j][:, mc, 0:cn],
                                  start=(mc == 0), stop=(mc == 1)) for mc in range(2)], reads=["Vp", ("PT", j)], writes=[("ps", 6 + dc % 2)])
                    S.op("dve", I("tensor_tensor", out=attT[:, 4 * h + dc, c0:c0 + cn], in0=pt[:, 0:cn], in1=rden[j][:, 0:cn], op=ALU.mult),
                         reads=[("ps", 6 + dc % 2), ("rden", j)], writes=[("attT", 4 * h + dc)])
        if blk == 1:
            sc0 = 448
            KTs = [A.alloc(f"KTs{i}", [128, 16, 256], BF16) for i in range(2)]
            Vs = [A.alloc(f"Vs{i}", [128, 2, D], BF16) for i in range(2)]
            PTs = A.alloc("PTs", [128, 4, 2, 128], BF16)
            rdens = A.alloc("rdens", [128, 4, 128], F32)
            PSA = [(PS[6], ("ps", 6)), (PS[7], ("ps", 7)), (PS[0], ("ps", 0)), (PS[1], ("ps", 1))]

            def load_kv(g):
                S.op("pool", I("dma_start", out=KTs[g % 2][:], in_=cKT_h.ap()[g]), writes=[("KTs", g % 2)], dma=f"kts{g % 2}")
                S.op("pool", I("dma_start", out=Vs[g % 2][:], in_=cV_h.ap()[g].rearrange("(mc mp) f -> mp mc f", mp=128)),
                     writes=[("Vs", g % 2)], dma=f"vs{g % 2}")

            load_kv(0)
            for g in range(NSEQ):
                if g + 1 < NSEQ:
                    load_kv(g + 1)
                kt = KTs[g % 2]
                vt = Vs[g % 2]
                for h in range(4):
                    for mc in range(2):
                        bank = PS[3 + (h // 2)]
                        col = ((h % 2) * 2 + mc) * 128 + 8 * g
                        S.op("pe", [I("matmul", bank[:, col:col + 8], lhsT=kt[:, 4 * h + dc, 128 * mc:128 * mc + 128],
                                      rhs=qT[:, 4 * h + dc, sc0 + 8 * g:sc0 + 8 * g + 8], start=(dc == 0), stop=(dc == 3), skip_group_check=True)
                                    for dc in range(4)], reads=[("KTs", g % 2)] + [("qT", 4 * h + dc) for dc in range(4)], writes=[("ps", 3 + h // 2)])
                for hb in range(2):
                    src = bcast(PS[3 + hb][:, 8 * g:8 * g + 1], [[128, 4], [1, 8]])
                    dst = bcast(PTs[:, 2 * hb, 0, 8 * g:8 * g + 1], [[128, 4], [1, 8]])
                    S.op("act", I("activation", out=dst, in_=src, func=AF.Exp, scale=ATT_SCALE), reads=[("ps", 3 + hb)], writes=["PTs"])
                for h in range(4):
                    S.op("pe", [I("matmul", PS[5][:, 128 * h + 8 * g:128 * h + 8 * g + 8], lhsT=onesb[:], rhs=PTs[:, h, mc, 8 * g:8 * g + 8],
                                  start=(mc == 0), stop=(mc == 1), skip_group_check=True) for mc in range(2)],
                         reads=["onesb", "PTs"], writes=[("ps", 5)])
                    pa, pak = PSA[h]
                    for dc in range(4):
                        S.op("pe", [I("matmul", pa[:, 128 * dc + 8 * g:128 * dc + 8 * g + 8], lhsT=vt[:, mc, 128 * (4 * h + dc):128 * (4 * h + dc) + 128],
                                      rhs=PTs[:, h, mc, 8 * g:8 * g + 8], start=(mc == 0), stop=(mc == 1), skip_group_check=True) for mc in range(2)],
                             reads=[("Vs", g % 2), "PTs"], writes=[pak])
            S.op("dve", I("reciprocal", out=rdens[:].rearrange("m h c -> m (h c)"), in_=PS[5][:, 0:512]), reads=[("ps", 5)], writes=["rdens"])
            for h in range(4):
                pa, pak = PSA[h]
                for dc in range(4):
                    S.op("dve", I("tensor_tensor", out=attT[:, 4 * h + dc, sc0:sc0 + 128], in0=pa[:, 128 * dc:128 * dc + 128], in1=rdens[:, h, :], op=ALU.mult),
                         reads=[pak, "rdens"], writes=[("attT", 4 * h + dc)])
        dump("q", qT, [("qT", k) for k in range(16)], t0)
        dump("att", attT, [("attT", k) for k in range(16)], t0)
        linear_resid_ln("wo", ws, KC, lambda kc, s0, sn: attT[:, kc, s0:s0 + sn], [("attT", k) for k in range(16)],
                        lambda ft: (XA[:, ft, :], [("XA", ft)]), V_LN2G, V_LN2B)
        dump("x2", XA, [("XA", k) for k in range(16)], t0)
        S.barrier()
        A.release(mB)
        mC = A.mark()
        hT = A.alloc("hT", [128, 64, TB], BF16)
        hr = [A.alloc(f"hr{i}", [128, 288], F32) for i in range(3)]
        for hc in range(64):
            wt, wkey = ws.next()
            for si, (s0, sn) in enumerate(SPL):
                b = lin_rot[0] % 3
                lin_rot[0] += 1
                pt = PS[b]
                mms = [I("matmul", pt[:, 0:sn], lhsT=wt[:, kc, :], rhs=XB[:, kc, s0:s0 + sn], start=(kc == 0), stop=(kc == KC - 1))
                       for kc in range(KC)]
                if hc == 0:
                    for kc in range(KC):
                        S.op("pe", mms[kc], reads=[wkey, ("XB", kc)], writes=[("ps", b)])
                else:
                    S.op("pe", mms, reads=[wkey] + [("XB", kc) for kc in range(16)], writes=[("ps", b)])
                S.op("act", I("activation", out=hr[b][:, 0:sn], in_=pt[:, 0:sn], func=AF.Relu), reads=[("ps", b)], writes=[("hr", b)])
                S.op("dve", I("tensor_tensor", out=hT[:, hc, s0:s0 + sn], in0=hr[b][:, 0:sn], in1=hr[b][:, 0:sn], op=ALU.mult),
                     reads=[("hr", b)], writes=[("hT", hc)])

        def final_out(ft, t0=t0):
            S.op("sp", I("dma_start", out=yT_o.ap()[128 * ft:128 * ft + 128, t0:t0 + TB], in_=XA[:, ft, :]), reads=[("XA", ft)], dma=f"yo{ft % 2}")

        linear_resid_ln("w2", ws, 64, lambda kc, s0, sn: hT[:, kc, s0:s0 + sn], [("hT", k) for k in range(64)],
                        lambda ft: (XA[:, ft, :], [("XA", ft)]), V_LN3G, V_LN3B, out_final=final_out, kpt=KC)
        S.barrier()
        A.release(mC)

    S.final_wait("sp")
    S.emit()
    return nc, S, A


def _tile_w(w, ncols_pad=None):
    din, dout = w.shape
    if ncols_pad is not None and ncols_pad > dout:
        w = np.concatenate([w, np.zeros((din, ncols_pad - dout), w.dtype)], axis=1)
        dout = ncols_pad
    return np.ascontiguousarray(w.reshape(din // 128, 128, dout // 128, 128).transpose(2, 1, 0, 3))


def _chunks(v, n):
    return np.ascontiguousarray(v.reshape(n, 128).T)


def make_in_maps(inp, cores=range(8)):
    f32 = np.float32
    x_prompt, x_sample = inp["x_prompt"], inp["x_sample"]
    win_t = _tile_w(inp["w_in"][0], N_WT * 128)
    cmat = np.zeros((128, 384), f32)
    cmat[:, 0:128] = np.eye(128, dtype=f32)
    cmat[0:64, 128:192] = 1.0
    cmat[64:128, 192:256] = 1.0
    cmat[:, 256:384] = 1.0
    mu = np.zeros(27 * 128, f32)
    mu[:3360] = inp["rwkv_mu"][0]
    vec_base = np.zeros((128, NV), f32)
    cw = inp["lru_conv_w"][0]
    for c in range(8):
        for j in range(4):
            vec_base[:, V_CW + c * 4 + j] = cw[j, 128 * c:128 * c + 128]
    vec_base[:, V_CB:V_CB + 8] = _chunks(inp["lru_conv_b"][0], 8)
    vec_base[:, V_BA:V_BA + 8] = _chunks(inp["lru_ba"][0], 8)
    vec_base[:, V_BX:V_BX + 8] = _chunks(inp["lru_bx"][0], 8)
    vec_base[:, V_LL:V_LL + 8] = _chunks(inp["lru_L"][0], 8)
    vec_base[:, V_MU:V_MU + 27] = _chunks(mu, 27)
    vec_base[:, V_W0:V_W0 + 8] = _chunks(inp["rwkv_w0"][0], 8)
    vec_base[:, V_A0:V_A0 + 8] = _chunks(inp["rwkv_a0"][0], 8)
    vec_base[:, V_KK:V_KK + 8] = _chunks(inp["rwkv_k_k"][0], 8)
    vec_base[:, V_KA:V_KA + 8] = _chunks(inp["rwkv_k_a"][0], 8)
    vec_base[:, V_RK:V_RK + 8] = _chunks(inp["rwkv_r_k"][0].reshape(-1), 8)
    vec_base[:, V_LNW:V_LNW + 8] = _chunks(inp["rwkv_ln_w"][0], 8)
    vec_base[:, V_LNB:V_LNB + 8] = _chunks(inp["rwkv_ln_b"][0], 8)
    for nm, col in (("ln1_g", V_LN1G), ("ln1_b", V_LN1B), ("ln2_g", V_LN2G), ("ln2_b", V_LN2B), ("ln3_g", V_LN3G), ("ln3_b", V_LN3B)):
        vec_base[:, col:col + 16] = _chunks(inp[nm][0], 16)
    lora_wa = np.ascontiguousarray(np.concatenate([inp["rwkv_w_up"][0], inp["rwkv_a_up"][0]], axis=0))
    gup = np.ascontiguousarray(inp["rwkv_g_up"][0])
    lru_w = np.ascontiguousarray(np.stack([inp["lru_wa"][0], inp["lru_wx"][0]]))
    wk_t = _tile_w(inp["xa_wk"][0])
    wv_r = np.ascontiguousarray(inp["xa_wv"][0].reshape(KC, 128, 4, 512).transpose(2, 1, 0, 3))
    wout_t = _tile_w(inp["w_out"][0])
    wq_t = _tile_w(inp["xa_wq"][0])
    wo_t = _tile_w(inp["xa_wo"][0])
    w1_t = _tile_w(inp["mlp_w1"][0])
    w2_t = np.ascontiguousarray(_tile_w(inp["mlp_w2"][0]).reshape(16, 128, 4, KC, 128).transpose(0, 2, 1, 3, 4))
    maps = []
    for c in cores:
        k, half = c // 2, c % 2
        xs = x_sample[16 * c:16 * c + 16].reshape(NS, D)
        if half == 0:
            pre = np.zeros((NPRE, D), f32)
            own = x_prompt[k, 0:NOWN]
        else:
            pre = x_prompt[k, 0:NPRE]
            own = x_prompt[k, NPRE:NPRE + NOWN]
        xT = np.ascontiguousarray(np.concatenate([pre, own, xs], axis=0).T)
        vec = vec_base.copy()
        vec[:, V_FLAG] = float(half)
        sh = np.zeros((NSEQ, 27 * 128), f32)
        sh[:, :3360] = inp["state_rwkv_shift"][0, 16 * c:16 * c + 16]
        shiftT = np.ascontiguousarray(sh.reshape(NSEQ, 27, 128).transpose(2, 1, 0))
        cv = inp["state_lru_conv"][0, 16 * c:16 * c + 16]
        convT = np.ascontiguousarray(cv.reshape(NSEQ, 3, 8, 128).transpose(3, 2, 0, 1))
        lh = inp["state_lru_h"][0, 16 * c:16 * c + 16]
        lruhT = np.ascontiguousarray(lh.reshape(NSEQ, 8, 128).transpose(2, 1, 0))
        Ss = inp["state_rwkv_S"][0, 16 * c:16 * c + 16]
        sS = np.ascontiguousarray(Ss.reshape(NSEQ, NPAIR, 2, 64, 64).transpose(0, 2, 4, 1, 3).reshape(NSEQ, 128, NPAIR, 64))
        memT = np.ascontiguousarray(inp["mem_prompt"][k].T)
        ck = inp["cache_mem_k"][0, 16 * c:16 * c + 16]
        cKT = np.ascontiguousarray(ck.reshape(NSEQ, 256, 4, 4, 128).transpose(0, 4, 2, 3, 1).reshape(NSEQ, 128, 16, 256))
        cV = np.ascontiguousarray(inp["cache_mem_v"][0, 16 * c:16 * c + 16].reshape(NSEQ, 256, D))
        maps.append({"xT": xT, "w_in_t": win_t, "vec": vec, "cmat": cmat, "shiftT": shiftT, "convT": convT,
                     "lruhT": lruhT, "lora_wa": lora_wa, "gup": gup, "lru_w": lru_w, "sS": sS,
                     "memT": memT, "wk_t": wk_t, "wv_r": wv_r, "wout_t": wout_t, "wq_t": wq_t, "wo_t": wo_t,
                     "w1_t": w1_t, "w2_t": w2_t, "cKT": cKT, "cV": cV})
    return maps


def assemble_outputs(results, cores=range(8)):
    f32 = np.float32
    y_prompt = np.zeros((4, T_P, D), f32)
    y_sample = np.zeros((128, TS, D), f32)
    mk = np.zeros((1, 4, 256, 4, 512), f32)
    mv = np.zeros((1, 4, 256, 4, 512), f32)
    lh_p = np.zeros((1, 4, LRU_W), f32)
    lc_p = np.zeros((1, 4, 3, LRU_W), f32)
    S_p = np.zeros((1, 4, 16, 64, 64), f32)
    sh_p = np.zeros((1, 4, 3360), f32)
    lh_s = np.zeros((1, 128, LRU_W), f32)
    lc_s = np.zeros((1, 128, 3, LRU_W), f32)
    S_s = np.zeros((1, 128, 16, 64, 64), f32)
    sh_s = np.zeros((1, 128, 3360), f32)
    for r, c in zip(results, cores):
        k, half = c // 2, c % 2
        yT = np.asarray(r["yT"])
        y_prompt[k, half * NOWN:(half + 1) * NOWN] = yT[:, :NOWN].T
        y_sample[16 * c:16 * c + 16] = yT[:, NOWN:].T.reshape(NSEQ, TS, D)
        lruh = np.asarray(r["lruh"]).transpose(1, 0, 2).reshape(LRU_W, 1 + NSEQ)
        lruc = np.asarray(r["lruc"]).transpose(1, 0, 2, 3).reshape(LRU_W, 1 + NSEQ, 3)
        shn = np.asarray(r["shn"]).transpose(1, 0, 2).reshape(27 * 128, 1 + NSEQ)[:3360]
        Ssd = np.asarray(r["S_s"]).reshape(NSEQ, 2, 64, NPAIR, 64).transpose(0, 3, 1, 4, 2).reshape(NSEQ, 16, 64, 64)
        lh_s[0, 16 * c:16 * c + 16] = lruh[:, 1:].T
        lc_s[0, 16 * c:16 * c + 16] = lruc[:, 1:, :].transpose(1, 2, 0)
        S_s[0, 16 * c:16 * c + 16] = Ssd
        sh_s[0, 16 * c:16 * c + 16] = shn[:, 1:].T
        if half == 0:
            mk[0, k] = np.asarray(r["mkT"]).T.reshape(256, 4, 512)
            mv[0, k] = np.asarray(r["mv"]).reshape(256, 4, 512)
        else:
            lh_p[0, k] = lruh[:, 0]
            lc_p[0, k] = lruc[:, 0, :].T
            S_p[0, k] = np.asarray(r["S_p"]).reshape(2, 64, NPAIR, 64).transpose(2, 0, 3, 1).reshape(16, 64, 64)
            sh_p[0, k] = shn[:, 0]
    return (y_prompt, y_sample, mk, mv, lh_p, lc_p, S_p, sh_p, lh_s, lc_s, S_s, sh_s)


def kernel(**inputs):
    inp = {k: np.asarray(v) for k, v in inputs.items()}
    maps = make_in_maps(inp)
    nc, S, A = build()
    res = run_bass_kernel_spmd(nc, maps, core_ids=list(range(8)))
    return assemble_outputs(res.results)
```

```python
import math
import numpy as np
import concourse.bass as bass
import concourse.mybir as mybir
from concourse.bass_utils import run_bass_kernel_spmd

F32 = mybir.dt.float32
BF16 = mybir.dt.bfloat16
AF = mybir.ActivationFunctionType
ALU = mybir.AluOpType

ENGINES = ("pe", "act", "dve", "pool", "sp")


def _is_psum_key(k):
    if isinstance(k, tuple):
        if k[0] in ("ps", "pj", "psH", "psU"):
            return True
    if isinstance(k, tuple):
        return False
    return isinstance(k, str) and k.startswith("p_")


class Sched:
    def __init__(self, nc, max_dma_streams=90):
        self.nc = nc
        self.q = {e: [] for e in ENGINES}
        self.sems = {}
        for e in ("pe", "act", "dve", "pool"):
            self.sems["E_" + e] = nc.alloc_semaphore("sem_" + e)
        self.val = {k: 0 for k in self.sems}
        self.seen = {e: {} for e in ENGINES}
        self.last_w = {}
        self.readers = {}
        self.max_dma_streams = max_dma_streams
        self.n_inst = 0

    def _stream_sem(self, stream):
        k = "D_" + stream
        if k not in self.sems:
            assert sum(1 for s in self.sems if s.startswith("D_")) < self.max_dma_streams, "too many dma streams"
            self.sems[k] = self.nc.alloc_semaphore("dsem_" + stream)
            self.val[k] = 0
        return k

    def op(self, eng, fns, reads=(), writes=(), dma=None):
        if callable(fns):
            fns = [fns]
        deps = {}

        def need(tok):
            if tok is None:
                return
            k, v = tok
            if k == "E_pe" and eng == "pe":
                return
            if self.seen[eng].get(k, 0) >= v:
                return
            if deps.get(k, 0) < v:
                deps[k] = v

        xr = [r for r in reads if _is_psum_key(r)]
        if xr:
            writes = list(writes) + [r for r in xr if r not in writes]
        own = "E_" + eng
        for r in reads:
            need(self.last_w.get(r))
        for w in writes:
            tok = self.last_w.get(w)
            if tok is not None and not (tok[0] == own and w not in reads):
                need(tok)
            for tok in self.readers.get(w, {}).items():
                need(tok)
        if dma is not None:
            k = self._stream_sem(eng + "_" + dma)
            if self.val[k] > 0:
                need((k, self.val[k]))
            self.val[k] += 16
            tok = (k, self.val[k])
            inc = (k, 16)
        else:
            k = "E_" + eng
            self.val[k] += 1
            tok = (k, self.val[k])
            inc = (k, 1)
        for k2, v2 in deps.items():
            self.seen[eng][k2] = v2
        self.q[eng].append((fns, list(deps.items()), inc))
        for w in writes:
            self.last_w[w] = tok
            self.readers[w] = {}
        for r in reads:
            d = self.readers.setdefault(r, {})
            if d.get(tok[0], 0) < tok[1]:
                d[tok[0]] = tok[1]
        self.n_inst += len(fns)
        return tok

    def barrier(self):
        for eng in ENGINES:
            deps = [(k, v) for k, v in self.val.items() if v > 0 and self.seen[eng].get(k, 0) < v
                    and not (k == "E_pe" and eng == "pe")]
            for k, v in deps:
                self.seen[eng][k] = v
            if deps:
                self.q[eng].append(([], deps, None))

    def final_wait(self, eng="sp"):
        deps = [(k, v) for k, v in self.val.items() if v > 0]
        self.q[eng].append(([], deps, None))

    def emit(self):
        nc, sems, q = self.nc, self.sems, self.q

        def run(e, lst):
            for fns, waits, inc in lst:
                for k, v in waits:
                    e.wait_ge(sems[k], v)
                inst = None
                for f in fns:
                    inst = f(e)
                if inc is not None:
                    inst.then_inc(sems[inc[0]], inc[1])

        with nc.Block() as blk:
            @blk.tensor
            def _(e):
                run(e, q["pe"])

            @blk.scalar
            def _(e):
                run(e, q["act"])

            @blk.vector
            def _(e):
                run(e, q["dve"])

            @blk.gpsimd
            def _(e):
                run(e, q["pool"])

            @blk.sync
            def _(e):
                run(e, q["sp"])


class Arena:
    def __init__(self, nc, base, top):
        self.nc, self.base, self.top, self.cur, self.n, self.peak = nc, base, top, base, 0, base

    def alloc(self, name, shape, dtype, align=64):
        size = 1
        for s in shape[1:]:
            size *= s
        nbytes = size * mybir.dt.size(dtype)
        off = (self.cur + align - 1) // align * align
        assert off + nbytes <= self.top, f"SBUF overflow allocating {name}: {off}+{nbytes} > {self.top}"
        self.cur = off + nbytes
        self.peak = max(self.peak, self.cur)
        self.n += 1
        return self.nc.alloc_sbuf_tensor_at(f"{name}_{self.n}", list(shape), dtype, offset=off)

    def mark(self):
        return self.cur

    def release(self, m):
        self.cur = m


def bcast(src_ap, dims):
    return bass.AP(src_ap.tensor, src_ap.offset, [list(src_ap.ap[0])] + [list(d) for d in dims])


D = 2048
KC = 16
T_P = 2048
NPRE = 1024
NOWN = 1024
NSEQ = 16
TS = 8
NS = NSEQ * TS
NTOK = NPRE + NOWN + NS
NOT = NOWN + NS
LRU_W = 1024
NPAIR = 8
P_TOTAL = 5408
N_WT = 43
EW = 2320
NE = 2193
SB0 = 2049
LW = 2240
DECAY_C = math.exp(-0.5)
CH = 128
ALPHA = 2.0 ** 0.25
LN_EPS = 1e-5
GN_EPS = 64e-5
TBLK = [(0, 512), (512, 512), (1024, 512), (1536, 512), (2048, 128)]
EBLK = [(0, 512), (512, 512), (1024, 512), (1536, 512), (2048, 256)]

V_CW, V_CB, V_BA, V_BX, V_LL, V_MU = 0, 32, 40, 48, 56, 64
V_W0, V_A0, V_KK, V_KA, V_RK, V_LNW, V_LNB = 91, 99, 107, 115, 123, 131, 139
V_LN1G, V_LN1B, V_LN2G, V_LN2B, V_LN3G, V_LN3B = 147, 163, 179, 195, 211, 227
V_FLAG = 243
NV = 244
DV_CL, DV_CL2, DV_OMKA = 0, 8, 16
NDV = 24


def I(method, *a, **kw):
    return lambda e: getattr(e, method)(*a, **kw)


class Ctx:
    pass


def build(debug=False, stop_after=None):
    nc = bass.Bass("TRN2", target_bir_lowering=False)
    S = Sched(nc)
    A = Arena(nc, 16512, 229300)
    C = Ctx()
    C.nc, C.S, C.A, C.debug = nc, S, A, debug
    build.last_C = C

    def din(name, shape, dt=F32):
        return nc.dram_tensor(name, list(shape), dt, kind="ExternalInput")

    def dout(name, shape, dt=F32):
        return nc.dram_tensor(name, list(shape), dt, kind="ExternalOutput")

    def dscr(name, shape, dt=F32):
        return nc.dram_tensor(name, list(shape), dt, kind=("ExternalOutput" if debug else "Internal"))

    C.din, C.dout, C.dscr = din, dout, dscr

    xT_h = din("xT", [D, NTOK])
    win_h = din("w_in_t", [N_WT, 128, KC, 128])
    vec_h = din("vec", [128, NV])
    cmat_h = din("cmat", [128, 384])
    shiftT_h = din("shiftT", [128, 27, NSEQ])
    convT_h = din("convT", [128, 8, NSEQ, 3])
    lruhT_h = din("lruhT", [128, 8, NSEQ])
    lora_h = din("lora_wa", [128, 1024])
    gup_h = din("gup", [160, 1024])
    lruw_h = din("lru_w", [2, 8, 128, 128])
    C.xT_h = xT_h

    shn_o = dout("shn", [128, 27, 1 + NSEQ])
    lruh_o = dout("lruh", [128, 8, 1 + NSEQ])
    lruc_o = dout("lruc", [128, 8, 1 + NSEQ, 3])

    C.L1_s = L1_s = dscr("L1_s", [128, NPAIR, EW, 4], BF16)
    C.w_s = w_s = dscr("w_s", [128, NPAIR, EW])
    C.bk_s = bk_s = dscr("bk_s", [2, 2, NPAIR, EW, 128], BF16)
    C.v_s = v_s = dscr("v_s", [2, NPAIR, EW, 64], BF16)
    C.G_s = G_s = dscr("G_s", [2, 128, NPAIR, EW])
    C.yT_s = yT_s = dscr("yT_s", [16, 128, NOT], BF16)

    vec = A.alloc("vec", [128, NV], F32)
    dv = A.alloc("dv", [128, NDV], F32)
    cmat = A.alloc("cmat", [128, 384], F32)
    identb = A.alloc("identb", [128, 128], BF16)
    C.vec, C.dv, C.cmat, C.identb = vec, dv, cmat, identb
    C.ident = ident = cmat[:, 0:128]
    C.blk1 = blk1 = cmat[:, 128:256]
    C.ones = ones = cmat[:, 256:384]

    S.op("sp", I("dma_start", out=vec[:], in_=vec_h.ap()), writes=["vec"], dma="c0")
    S.op("sp", I("dma_start", out=cmat[:], in_=cmat_h.ap()), writes=["cmat"], dma="c1")
    S.op("dve", I("tensor_copy", out=identb[:], in_=cmat[:, 0:128]), reads=["cmat"], writes=["identb"])
    S.op("act", I("activation", out=dv[:, 0:8], in_=vec[:, V_LL:V_LL + 8], func=AF.Exp, scale=-1.0), reads=["vec"], writes=["dv"])
    S.op("act", I("activation", out=dv[:, 0:8], in_=dv[:, 0:8], func=AF.Ln, bias=1.0), reads=["dv"], writes=["dv"])
    S.op("dve", I("tensor_scalar", out=dv[:, 8:16], in0=dv[:, 0:8], scalar1=-16.0, scalar2=None, op0=ALU.mult), reads=["dv"], writes=["dv"])
    S.op("dve", I("tensor_scalar", out=dv[:, 0:8], in0=dv[:, 0:8], scalar1=-8.0, scalar2=None, op0=ALU.mult), reads=["dv"], writes=["dv"])
    S.op("dve", I("tensor_scalar", out=dv[:, 16:24], in0=vec[:, V_KA:V_KA + 8], scalar1=-1.0, scalar2=1.0, op0=ALU.mult, op1=ALU.add),
         reads=["vec"], writes=["dv"])

    def vcol(i):
        return vec[:, i:i + 1]

    def dvcol(i):
        return dv[:, i:i + 1]

    C.vcol, C.dvcol = vcol, dvcol
    C.PS = PS = [nc.alloc_psum_tensor(f"ps{i}", [128, 512], F32) for i in range(8)]

    m_phase1 = A.mark()
    xb = A.alloc("xb", [128, KC, NTOK], BF16)
    xT_v = xT_h.ap().rearrange("(kc kp) t -> kp kc t", kp=128)
    def load_xb(bi):
        c0, n = TBLK[bi]
        S.op("pool", I("dma_start", out=xb[:, :, c0:c0 + n], in_=xT_v[:, :, c0:c0 + n]), writes=[("xb", bi)], dma=f"xb{bi % 2}")

    NWS = 4
    wslot = [A.alloc(f"win{i}", [128, KC, 128], BF16) for i in range(NWS)]
    wt_loaded = {}
    wt_order = [40, 41, 42]
    for p in range(NPAIR):
        wt_order += [16 + p, 24 + p, 32 + p]
    for c in range(8):
        wt_order += [c, 8 + c]
    wt_issue = [0]

    def issue_wload():
        i = wt_issue[0]
        if i >= len(wt_order):
            return
        wt = wt_order[i]
        sl = i % NWS
        S.op("pool", I("dma_start", out=wslot[sl][:], in_=win_h.ap()[wt]), writes=[("win", sl)], dma=f"win{sl}")
        wt_loaded[wt] = sl
        wt_issue[0] += 1

    issue_wload()
    load_xb(0)
    load_xb(1)
    issue_wload()
    issue_wload()
    load_xb(2)
    load_xb(3)
    load_xb(4)

    pj_rot = [0]

    def project(wt, evac, blocks=(0, 1, 2, 3, 4)):
        issue_wload()
        sl = wt_loaded[wt]
        for bi in blocks:
            c0, n = TBLK[bi]
            b = pj_rot[0] % 2
            pj_rot[0] += 1
            pt = PS[b]
            fns = [I("matmul", pt[:, 0:n], lhsT=wslot[sl][:, kc, :], rhs=xb[:, kc, c0:c0 + n], start=(kc == 0), stop=(kc == KC - 1))
                   for kc in range(KC)]
            S.op("pe", fns, reads=[("win", sl), ("xb", bi)], writes=[("pj", b)])
            evac(bi, pt, b, c0, n)

    def project_deferred(wt, blocks, banks):
        issue_wload()
        sl = wt_loaded[wt]
        out = []
        for bi, bk_ in zip(blocks, banks):
            c0, n = TBLK[bi]
            pt = PS[bk_]
            fns = [I("matmul", pt[:, 0:n], lhsT=wslot[sl][:, kc, :], rhs=xb[:, kc, c0:c0 + n], start=(kc == 0), stop=(kc == KC - 1))
                   for kc in range(KC)]
            S.op("pe", fns, reads=[("win", sl), ("xb", bi)], writes=[("pj", bk_)])
            out.append((bi, pt, bk_, c0, n))
        return out

    m_rwkv = A.mark()
    shiftT = A.alloc("shiftT", [128, 27, NSEQ], F32)
    S.op("sp", I("dma_start", out=shiftT[:], in_=shiftT_h.ap()), writes=["shiftT"], dma="c2")
    shn = A.alloc("shn", [128, 27, 1 + NSEQ], F32)
    lora = A.alloc("lora", [128, 1024], BF16)
    gup1 = A.alloc("gup1", [128, 1024], BF16)
    gup2 = A.alloc("gup2", [32, 1024], BF16)
    S.op("pool", I("dma_start", out=lora[:], in_=lora_h.ap()), writes=["lora"], dma="c3")
    S.op("pool", I("dma_start", out=gup1[:], in_=gup_h.ap()[0:128]), writes=["gup1"], dma="c4")
    S.op("pool", I("dma_start", out=gup2[:], in_=gup_h.ap()[128:160]), writes=["gup2"], dma="c5")
    codes = [A.alloc(f"codes{i}", [128, EW], BF16) for i in range(3)]
    Usets = [[A.alloc(f"U{s_}_{i}", [128, EW], F32) for i in range(3)] for s_ in range(2)]
    DW = 580
    dtmp = A.alloc("dtmp", [128, DW], F32)
    for s_ in range(2):
        for i in range(3):
            S.op("pool", I("memset", Usets[s_][i][:], 0.0), writes=[("U", s_, i)])
    cur = {"set": 0}

    def evac_rwkv(ui, st=0):
        def f(bi, pt, b, c0, n):
            u = Usets[st][ui]
            if bi < 4:
                S.op("act", I("activation", out=u[:, 1 + c0:1 + c0 + n], in_=pt[:, 0:n], func=AF.Copy),
                     reads=[("pj", b)], writes=[("U", st, ui)])
            else:
                dst = bcast(u[:, SB0 + 1:SB0 + 2], [[9, NSEQ], [1, TS]])
                S.op("act", I("activation", out=dst, in_=pt[:, 0:NS].rearrange("p (g s) -> p g s", s=TS), func=AF.Copy),
                     reads=[("pj", b)], writes=[("U", st, ui)])
        return f

    def shift_ops(ui, ti, st=0):
        u = Usets[st][ui]
        key = ("U", st, ui)
        dst = bcast(u[:, SB0:SB0 + 1], [[9, NSEQ]])
        S.op("pool", I("tensor_copy", out=dst, in_=shiftT[:, ti, :]), reads=["shiftT", key], writes=[key]); yield
        S.op("pool", I("tensor_copy", out=shn[:, ti, 0:1], in_=u[:, 2048:2049]), reads=[key], writes=["shn"]); yield
        src = bcast(u[:, SB0 + TS:SB0 + TS + 1], [[9, NSEQ]])
        S.op("pool", I("tensor_copy", out=shn[:, ti, 1:1 + NSEQ], in_=src), reads=[key], writes=["shn"]); yield
        for c1 in range(EW, 1, -DW):
            c0 = max(1, c1 - DW)
            n = c1 - c0
            S.op("pool", I("tensor_tensor", out=dtmp[:, 0:n], in0=u[:, c0 - 1:c1 - 1], in1=u[:, c0:c1], op=ALU.subtract),
                 reads=[key], writes=["dtmp"]); yield
            S.op("dve", I("scalar_tensor_tensor", out=u[:, c0:c1], in0=dtmp[:, 0:n], scalar=vcol(V_MU + ti), in1=u[:, c0:c1],
                          op0=ALU.mult, op1=ALU.add), reads=["dtmp", key, "vec"], writes=[key]); yield

    def shift_tile(ui, ti, st=0):
        for _ in shift_ops(ui, ti, st):
            pass

    project(40, evac_rwkv(0))
    shift_tile(0, 24)
    S.op("act", I("activation", out=codes[0][0:64, :], in_=Usets[0][0][0:64, :], func=AF.Tanh), reads=[("U", 0, 0)], writes=["codes0"])
    S.op("act", I("activation", out=codes[0][64:128, :], in_=Usets[0][0][64:128, :], func=AF.Copy), reads=[("U", 0, 0)], writes=["codes0"])
    project(41, evac_rwkv(1))
    shift_tile(1, 25)
    S.op("act", I("activation", out=codes[1][:], in_=Usets[0][1][:], func=AF.Sigmoid), reads=[("U", 0, 1)], writes=["codes1"])
    project(42, evac_rwkv(2))
    shift_tile(2, 26)
    S.op("act", I("activation", out=codes[2][:], in_=Usets[0][2][:], func=AF.Sigmoid), reads=[("U", 0, 2)], writes=["codes2"])

    BW = 128
    NPAR = 4

    def mk_temps(k):
        T = {}
        for nm in ("tA", "tW", "tK", "tQ", "tT", "tR", "tG1", "tG2", "tL", "tC", "tP"):
            T[nm] = A.alloc(f"{nm}{k}", [128, BW], F32)
        for nm in ("bB", "bK", "bV"):
            T[nm] = A.alloc(f"{nm}{k}", [128, BW], BF16)
        T["L1st"] = A.alloc(f"L1st{k}", [128, BW, 4], BF16)
        T["stBK"] = A.alloc(f"stBK{k}", [128, 2, 2, 128], BF16)
        T["stV"] = A.alloc(f"stV{k}", [128, 128], BF16)
        S.op("pool", I("memset", T["L1st"][:], 0.0), writes=[("L1st", k)])
        S.op("pool", I("memset", T["stBK"][:], 0.0), writes=[("stBK", k)])
        return T

    TT = [mk_temps(k) for k in range(NPAR)]
    ccar = A.alloc("ccar", [128, 1], F32)
    keepE = A.alloc("keepE", [128, EW], BF16)
    S.op("pool", I("memset", keepE[:], 1.0), writes=["keepE"])
    S.op("pool", I("memset", bcast(keepE[:, 0:1], [[CH, 2048 // CH + 1]]), 0.0), reads=["keepE"], writes=["keepE"])
    S.op("pool", I("memset", bcast(keepE[:, SB0:SB0 + 1], [[9, NSEQ]]), 0.0), reads=["keepE"], writes=["keepE"])
    S.op("pool", I("memset", bcast(keepE[:, SB0 + TS:SB0 + TS + 1], [[9, NSEQ]]), 0.0), reads=["keepE"], writes=["keepE"])
    P_WL, P_AL, P_GL, P_SS, P_TR = PS[2], PS[3], PS[4], PS[5], PS[6]
    P_TRb = P_TR.bitcast(BF16)
    P_TRb2 = PS[7].bitcast(BF16)
    SUBS = [(BW * i, BW) for i in range(2304 // BW)]

    def block_gen(p, e0, bw, k, st):
        T = TT[k]
        tA, tW, tK, tQ, tT, tR, tG1, tG2, tL, tC, tP = (T[n] for n in ("tA", "tW", "tK", "tQ", "tT", "tR", "tG1", "tG2", "tL", "tC", "tP"))
        bB, bK, bV, L1st, stBK, stV = (T[n] for n in ("bB", "bK", "bV", "L1st", "stBK", "stV"))
        K = lambda nm: (nm, k)
        Ur, Uk, Uv = Usets[st]
        ch = slice(128 * p, 128 * p + 128)
        sf = slice(e0 + 1, e0 + 1 + bw)
        ef = slice(e0, e0 + bw)
        own = e0 >= 1024
        w_ = slice(0, bw)
        pc = slice(BW * k, BW * k + bw)
        S.op("pe", I("matmul", P_WL[:, pc], lhsT=lora[0:64, ch], rhs=codes[0][0:64, sf], start=True, stop=True),
             reads=["lora", "codes0"], writes=["p_wl"]); yield
        S.op("pe", I("matmul", P_AL[:, pc], lhsT=lora[64:128, ch], rhs=codes[0][64:128, sf], start=True, stop=True),
             reads=["lora", "codes0"], writes=["p_al"]); yield
        if own:
            S.op("pe", [I("matmul", P_GL[:, pc], lhsT=gup1[:, ch], rhs=codes[1][:, sf], start=True, stop=False, skip_group_check=True),
                        I("matmul", P_GL[:, pc], lhsT=gup2[0:32, ch], rhs=codes[2][0:32, sf], start=False, stop=True, skip_group_check=True)],
                 reads=["gup1", "gup2", "codes1", "codes2"], writes=["p_gl"]); yield
        S.op("act", I("activation", out=tA[:, w_], in_=P_AL[:, pc], func=AF.Sigmoid, bias=vcol(V_A0 + p)), reads=["p_al", "vec"], writes=[K("tA")]); yield
        S.op("act", I("activation", out=tW[:, w_], in_=P_WL[:, pc], func=AF.Sigmoid, bias=vcol(V_W0 + p)), reads=["p_wl", "vec"], writes=[K("tW")]); yield
        S.op("dve", I("tensor_scalar", out=tL[:, w_], in0=tW[:, w_], scalar1=-DECAY_C, scalar2=None, op0=ALU.mult), reads=[K("tW")], writes=[K("tL")]); yield
        carry_in = (e0 == 2176)
        S.op("dve", I("tensor_tensor_scan", out=tC[:, w_], data0=keepE[:, ef], data1=tL[:, w_], initial=(ccar[:, 0:1] if carry_in else 0.0),
                      op0=ALU.mult, op1=ALU.add), reads=["keepE", K("tL")] + (["ccar"] if carry_in else []), writes=[K("tC")]); yield
        if e0 == 2048:
            S.op("dve", I("tensor_copy", out=ccar[:, 0:1], in_=tC[:, bw - 1:bw]), reads=[K("tC")], writes=["ccar"]); yield
        S.op("pool", I("tensor_tensor", out=tP[:, w_], in0=tC[:, w_], in1=tL[:, w_], op=ALU.subtract), reads=[K("tC"), K("tL")], writes=[K("tP")]); yield
        S.op("act", I("activation", out=tP[:, w_], in_=tP[:, w_], func=AF.Exp), reads=[K("tP")], writes=[K("tP")]); yield
        S.op("act", I("activation", out=tW[:, w_], in_=tC[:, w_], func=AF.Exp), reads=[K("tC")], writes=[K("tW")]); yield
        S.op("act", I("activation", out=tC[:, w_], in_=tC[:, w_], func=AF.Exp, scale=-1.0), reads=[K("tC")], writes=[K("tC")]); yield
        S.op("sp", I("dma_start", out=w_s.ap()[:, p, e0:e0 + bw], in_=tW[:, w_]), reads=[K("tW")], writes=["w_s"], dma=f"ws{k}"); yield
        S.op("dve", I("tensor_scalar", out=tK[:, w_], in0=Uk[:, sf], scalar1=vcol(V_KK + p), scalar2=None, op0=ALU.mult),
             reads=[("U", st, 1), "vec"], writes=[K("tK")]); yield
        S.op("act", I("activation", out=tQ[:, w_], in_=tK[:, w_], func=AF.Square), reads=[K("tK")], writes=[K("tQ")]); yield
        S.op("pe", I("matmul", P_SS[:, pc], lhsT=blk1, rhs=tQ[:, w_], start=True, stop=True), reads=["cmat", K("tQ")], writes=["p_ss"]); yield
        S.op("dve", I("tensor_scalar", out=tQ[:, w_], in0=P_SS[:, pc], scalar1=1e-24, scalar2=None, op0=ALU.max), reads=["p_ss"], writes=[K("tQ")]); yield
        S.op("act", I("activation", out=tQ[:, w_], in_=tQ[:, w_], func=AF.Sqrt), reads=[K("tQ")], writes=[K("tQ")]); yield
        S.op("dve", I("reciprocal", out=tQ[:, w_], in_=tQ[:, w_]), reads=[K("tQ")], writes=[K("tQ")]); yield
        S.op("dve", I("tensor_tensor", out=tK[:, w_], in0=tK[:, w_], in1=tQ[:, w_], op=ALU.mult), reads=[K("tK"), K("tQ")], writes=[K("tK")]); yield
        S.op("dve", I("tensor_tensor", out=tQ[:, w_], in0=tK[:, w_], in1=tA[:, w_], op=ALU.mult), reads=[K("tK"), K("tA")], writes=[K("tQ")]); yield
        S.op("pool", I("tensor_tensor", out=bB[:, w_], in0=tQ[:, w_], in1=tC[:, w_], op=ALU.mult), reads=[K("tQ"), K("tC")], writes=[K("bB")]); yield
        S.op("dve", I("tensor_scalar", out=tT[:, w_], in0=tA[:, w_], scalar1=vcol(V_KA + p), scalar2=dvcol(DV_OMKA + p), op0=ALU.mult, op1=ALU.add),
             reads=[K("tA"), "vec", "dv"], writes=[K("tT")]); yield
        S.op("dve", I("tensor_tensor", out=tT[:, w_], in0=tT[:, w_], in1=Uk[:, sf], op=ALU.mult), reads=[K("tT"), ("U", st, 1)], writes=[K("tT")]); yield
        S.op("pool", I("tensor_tensor", out=bK[:, w_], in0=tT[:, w_], in1=tC[:, w_], op=ALU.mult), reads=[K("tT"), K("tC")], writes=[K("bK")]); yield
        S.op("act", I("activation", out=bV[:, w_], in_=Uv[:, sf], func=AF.Copy), reads=[("U", st, 2)], writes=[K("bV")]); yield
        S.op("pool", I("tensor_tensor", out=L1st[0:64, w_, 0], in0=tK[0:64, w_], in1=tP[0:64, w_], op=ALU.mult), reads=[K("tK"), K("tP")], writes=[K("L1st")]); yield
        S.op("pool", I("tensor_tensor", out=L1st[64:128, w_, 1], in0=tK[64:128, w_], in1=tP[64:128, w_], op=ALU.mult), reads=[K("tK"), K("tP")], writes=[K("L1st")]); yield
        S.op("dve", I("tensor_tensor", out=L1st[0:64, w_, 2], in0=Ur[0:64, ef], in1=tP[0:64, w_], op=ALU.mult), reads=[("U", st, 0), K("tP")], writes=[K("L1st")]); yield
        S.op("dve", I("tensor_tensor", out=L1st[64:128, w_, 3], in0=Ur[64:128, ef], in1=tP[64:128, w_], op=ALU.mult), reads=[("U", st, 0), K("tP")], writes=[K("L1st")]); yield
        S.op("sp", I("dma_start", out=L1_s.ap()[:, p, e0:e0 + bw, :], in_=L1st[:, w_, :]), reads=[K("L1st")], writes=["L1_s"], dma=f"l1s{k}"); yield
        if own:
            S.op("dve", I("scalar_tensor_tensor", out=tR[:, w_], in0=Ur[:, sf], scalar=vcol(V_RK + p), in1=tT[:, w_], op0=ALU.mult, op1=ALU.mult),
                 reads=[("U", st, 0), K("tT"), "vec"], writes=[K("tR")]); yield
            S.op("pe", I("matmul", P_SS[:, pc], lhsT=blk1, rhs=tR[:, w_], start=True, stop=True), reads=["cmat", K("tR")], writes=["p_ss"]); yield
            S.op("dve", I("tensor_tensor", out=tR[:, w_], in0=P_SS[:, pc], in1=Uv[:, sf], op=ALU.mult), reads=["p_ss", ("U", st, 2)], writes=[K("tR")]); yield
            S.op("act", I("activation", out=tG1[:, w_], in_=P_GL[:, pc], func=AF.Copy, scale=vcol(V_LNW + p)), reads=["p_gl", "vec"], writes=[K("tG1")]); yield
            S.op("dve", I("scalar_tensor_tensor", out=tG2[:, w_], in0=tR[:, w_], scalar=vcol(V_LNB + p), in1=P_GL[:, pc], op0=ALU.add, op1=ALU.mult),
                 reads=[K("tR"), "p_gl", "vec"], writes=[K("tG2")]); yield
            S.op("sp", I("dma_start", out=G_s.ap()[0, :, p, e0:e0 + bw], in_=tG1[:, w_]), reads=[K("tG1")], writes=["G_s"], dma=f"g1s{k}"); yield
            S.op("sp", I("dma_start", out=G_s.ap()[1, :, p, e0:e0 + bw], in_=tG2[:, w_]), reads=[K("tG2")], writes=["G_s"], dma=f"g2s{k}"); yield
        for sb in range(bw // 128):
            cs = slice(sb * 128, sb * 128 + 128)
            eb = e0 + sb * 128
            tb = 384 * (k % 2)
            P_T = P_TRb if k < 2 else P_TRb2
            ptk = "p_tr" if k < 2 else "p_tr2"
            S.op("pe", [I("transpose", P_T[:, tb:tb + 128], bB[:, cs], identb[:]),
                        I("transpose", P_T[:, tb + 128:tb + 256], bK[:, cs], identb[:]),
                        I("transpose", P_T[:, tb + 256:tb + 384], bV[:, cs], identb[:])],
                 reads=[K("bB"), K("bK"), K("bV"), "identb"], writes=[ptk]); yield
            for wh in range(2):
                src = bcast(P_T[:, tb + wh * 128:tb + wh * 128 + 1], [[64, 2], [1, 64]])
                dst = bcast(stBK[:, wh, 0, 0:1], [[192, 2], [1, 64]])
                S.op("act", I("activation", out=dst, in_=src, func=AF.Copy, scale=(-1.0 if wh == 0 else 1.0)), reads=[ptk], writes=[K("stBK")]); yield
            S.op("act", I("activation", out=stV[:], in_=P_T[:, tb + 256:tb + 384], func=AF.Copy), reads=[ptk], writes=[K("stV")]); yield
            S.op("sp", I("dma_start", out=bk_s.ap()[:, :, p, eb:eb + 128, :].rearrange("w r t j -> t w r j"), in_=stBK[:]),
                 reads=[K("stBK")], writes=["bk_s"], dma=f"bks{k}"); yield
            S.op("sp", I("dma_start", out=v_s.ap()[:, p, eb:eb + 128, :].rearrange("r t i -> t r i"), in_=stV[:].rearrange("t (r i) -> t r i", r=2)),
                 reads=[K("stV")], writes=["v_s"], dma=f"vs{k}"); yield

    def run_interleaved(gens):
        active = list(gens)
        while active:
            for g in list(active):
                try:
                    next(g)
                except StopIteration:
                    active.remove(g)

    def project_gen(wt, evac, blocks=(0, 1, 2, 3, 4)):
        issue_wload()
        sl = wt_loaded[wt]
        for bi in blocks:
            c0, n = TBLK[bi]
            b = pj_rot[0] % 2
            pj_rot[0] += 1
            pt = PS[b]
            fns = [I("matmul", pt[:, 0:n], lhsT=wslot[sl][:, kc, :], rhs=xb[:, kc, c0:c0 + n], start=(kc == 0), stop=(kc == KC - 1))
                   for kc in range(KC)]
            S.op("pe", fns, reads=[("win", sl), ("xb", bi)], writes=[("pj", b)])
            evac(bi, pt, b, c0, n)
            yield

    def pair_proj_gen(p, st):
        for ui, wt0, ti0 in ((0, 16, 0), (1, 24, 8), (2, 32, 16)):
            for _ in project_gen(wt0 + p, evac_rwkv(ui, st)):
                yield
            for _ in shift_ops(ui, ti0 + p, st):
                yield

    def chains_gen(p, st):
        groups_ = [SUBS[i0:i0 + NPAR] for i0 in range(0, 16, NPAR)] + [[SUBS[16]], [SUBS[17]]]
        for grp in groups_:
            gens = [block_gen(p, e0, bw, k, st) for k, (e0, bw) in enumerate(grp)]
            active = list(gens)
            while active:
                for g in list(active):
                    try:
                        next(g)
                    except StopIteration:
                        active.remove(g)
                yield

    for _ in pair_proj_gen(0, 0):
        pass
    for p in range(NPAIR):
        st = p % 2
        cg = chains_gen(p, st)
        pg = pair_proj_gen(p + 1, 1 - st) if p + 1 < NPAIR else iter(())
        rounds = 0
        pg_done = False
        for _ in cg:
            rounds += 1
            if not pg_done and rounds % 4 == 0:
                try:
                    next(pg)
                except StopIteration:
                    pg_done = True
        for _ in pg:
            pass

    S.op("sp", I("dma_start", out=shn_o.ap(), in_=shn[:]), reads=["shn"], dma="o_shn")
    S.barrier()
    A.release(m_rwkv)

    convT = A.alloc("convT", [128, 8, NSEQ, 3], F32)
    lruhT = A.alloc("lruhT", [128, 8, NSEQ], F32)
    lruw = A.alloc("lruw", [128, 2, 8, 128], BF16)
    lruh = A.alloc("lruh", [128, 8, 1 + NSEQ], F32)
    lruc = A.alloc("lruc", [128, 8, 1 + NSEQ, 3], F32)
    S.op("sp", I("dma_start", out=convT[:], in_=convT_h.ap()), writes=["convT"], dma="c2")
    S.op("sp", I("dma_start", out=lruhT[:], in_=lruhT_h.ap()), writes=["lruhT"], dma="c2")
    S.op("pool", I("dma_start", out=lruw[:], in_=lruw_h.ap().rearrange("w n c d -> c w n d")), writes=["lruw"], dma="c3")
    Ux = A.alloc("Ux", [128, LW], F32)
    xc = A.alloc("xc", [128, LW], F32)
    xcb = A.alloc("xcb", [128, LW], BF16)
    Rg = A.alloc("Rg", [128, LW], F32)
    Ig = A.alloc("Ig", [128, LW], F32)
    Mg = A.alloc("Mg", [128, LW], F32)
    Hh = A.alloc("Hh", [128, LW], F32)
    Gt = A.alloc("Gt", [128, NOT], F32)
    G2t = A.alloc("G2t", [128, NOT], F32)
    yl = A.alloc("yl", [128, NOT], BF16)
    h0o = A.alloc("h0o", [128, 1], F32)
    S.op("pool", I("memset", Ux[:], 0.0), writes=["Ux"])
    LB = [(3, 512), (515, 512), (1027, 512), (1539, 512), (2051, 176)]
    SP0 = 2051
    LV = 2227

    def evac_x(bi, pt, b, c0, n):
        if bi < 4:
            S.op("act", I("activation", out=Ux[:, 3 + c0:3 + c0 + n], in_=pt[:, 0:n], func=AF.Copy), reads=[("pj", b)], writes=["Ux"])
        else:
            dst = bcast(Ux[:, SP0 + 3:SP0 + 4], [[11, NSEQ], [1, TS]])
            S.op("act", I("activation", out=dst, in_=pt[:, 0:NS].rearrange("p (g s) -> p g s", s=TS), func=AF.Copy),
                 reads=[("pj", b)], writes=["Ux"])

    def evac_g(bi, pt, b, c0, n):
        o0 = c0 - NPRE
        S.op("act", I("activation", out=Gt[:, o0:o0 + n], in_=pt[:, 0:n], func=AF.Copy), reads=[("pj", b)], writes=["Gt"])

    for c in range(8):
        project(c, evac_x)
        S.op("pool", I("tensor_copy", out=bcast(Ux[:, SP0:SP0 + 1], [[11, NSEQ], [1, 3]]), in_=convT[:, c, :, :]),
             reads=["convT", "Ux"], writes=["Ux"])
        S.op("pool", I("tensor_copy", out=lruc[:, c, 0, :], in_=Ux[:, 2048:2051]), reads=["Ux"], writes=["lruc"])
        S.op("pool", I("tensor_copy", out=lruc[:, c, 1:1 + NSEQ, :], in_=bcast(Ux[:, SP0 + 8:SP0 + 9], [[11, NSEQ], [1, 3]])),
             reads=["Ux"], writes=["lruc"])
        W3 = LW - 3
        S.op("dve", I("tensor_scalar", out=xc[:, 3:LV], in0=Ux[:, 3:LV], scalar1=vcol(V_CW + 4 * c + 3), scalar2=vcol(V_CB + c),
                      op0=ALU.mult, op1=ALU.add), reads=["Ux", "vec"], writes=["xc"])
        for j in (1, 2, 3):
            S.op("dve", I("scalar_tensor_tensor", out=xc[:, 3:LV], in0=Ux[:, 3 - j:LV - j], scalar=vcol(V_CW + 4 * c + 3 - j),
                          in1=xc[:, 3:LV], op0=ALU.mult, op1=ALU.add), reads=["Ux", "xc", "vec"], writes=["xc"])
        S.op("act", I("activation", out=xcb[:, 3:LV], in_=xc[:, 3:LV], func=AF.Copy), reads=["xc"], writes=["xcb"])
        for (l0, n) in LB:
            S.op("pe", I("matmul", PS[2][:, 0:n], lhsT=lruw[:, 0, c, :], rhs=xcb[:, l0:l0 + n], start=True, stop=True),
                 reads=["lruw", "xcb"], writes=["p_wl"])
            S.op("act", I("activation", out=Rg[:, l0:l0 + n], in_=PS[2][:, 0:n], func=AF.Sigmoid, bias=vcol(V_BA + c)),
                 reads=["p_wl", "vec"], writes=["Rg"])
            S.op("pe", I("matmul", PS[3][:, 0:n], lhsT=lruw[:, 1, c, :], rhs=xcb[:, l0:l0 + n], start=True, stop=True),
                 reads=["lruw", "xcb"], writes=["p_al"])
            S.op("act", I("activation", out=Ig[:, l0:l0 + n], in_=PS[3][:, 0:n], func=AF.Sigmoid, bias=vcol(V_BX + c)),
                 reads=["p_al", "vec"], writes=["Ig"])
        S.op("act", I("activation", out=Mg[:, 3:LV], in_=Rg[:, 3:LV], func=AF.Exp, scale=dvcol(DV_CL2 + c)), reads=["Rg", "dv"], writes=["Mg"])
        S.op("act", I("activation", out=Rg[:, 3:LV], in_=Rg[:, 3:LV], func=AF.Exp, scale=dvcol(DV_CL + c)), reads=["Rg", "dv"], writes=["Rg"])
        S.op("pool", I("tensor_scalar", out=Mg[:, 3:LV], in0=Mg[:, 3:LV], scalar1=-1.0, scalar2=1.0, op0=ALU.mult, op1=ALU.add),
             reads=["Mg"], writes=["Mg"])
        S.op("act", I("activation", out=Mg[:, 3:LV], in_=Mg[:, 3:LV], func=AF.Sqrt), reads=["Mg"], writes=["Mg"])
        S.op("pool", I("tensor_tensor", out=Ig[:, 3:LV], in0=Ig[:, 3:LV], in1=xc[:, 3:LV], op=ALU.mult), reads=["Ig", "xc"], writes=["Ig"])
        S.op("dve", I("tensor_tensor", out=Ig[:, 3:LV], in0=Ig[:, 3:LV], in1=Mg[:, 3:LV], op=ALU.mult), reads=["Ig", "Mg"], writes=["Ig"])
        S.op("pool", I("memset", bcast(Rg[:, SP0:SP0 + 1], [[11, NSEQ], [1, 3]]), 0.0), reads=["Rg"], writes=["Rg"])
        S.op("pool", I("memset", bcast(Ig[:, SP0:SP0 + 1], [[11, NSEQ], [1, 3]]), 0.0), reads=["Ig"], writes=["Ig"])
        S.op("pool", I("tensor_copy", out=bcast(Ig[:, SP0 + 2:SP0 + 3], [[11, NSEQ]]), in_=lruhT[:, c, :]), reads=["Ig", "lruhT"], writes=["Ig"])
        S.op("dve", I("tensor_tensor_scan", out=Hh[:, 3:1027], data0=Rg[:, 3:1027], data1=Ig[:, 3:1027], initial=0.0,
                      op0=ALU.mult, op1=ALU.add), reads=["Rg", "Ig"], writes=["Hh"])
        S.op("dve", I("tensor_tensor", out=h0o[:], in0=Hh[:, 1026:1027], in1=vcol(V_FLAG), op=ALU.mult), reads=["Hh", "vec"], writes=["h0o"])
        S.op("dve", I("tensor_tensor_scan", out=Hh[:, 1027:2051], data0=Rg[:, 1027:2051], data1=Ig[:, 1027:2051], initial=h0o[:, 0:1],
                      op0=ALU.mult, op1=ALU.add), reads=["Rg", "Ig", "h0o", "Hh"], writes=["Hh"])
        S.op("dve", I("tensor_tensor_scan", out=Hh[:, SP0:SP0 + 176], data0=Rg[:, SP0:SP0 + 176], data1=Ig[:, SP0:SP0 + 176], initial=0.0,
                      op0=ALU.mult, op1=ALU.add), reads=["Rg", "Ig", "Hh"], writes=["Hh"])
        S.op("pool", I("tensor_copy", out=lruh[:, c, 0:1], in_=Hh[:, 2050:2051]), reads=["Hh"], writes=["lruh"])
        S.op("pool", I("tensor_copy", out=lruh[:, c, 1:1 + NSEQ], in_=bcast(Hh[:, SP0 + 10:SP0 + 11], [[11, NSEQ]])), reads=["Hh"], writes=["lruh"])
        project(8 + c, evac_g, blocks=(2, 3, 4))
        S.op("act", I("activation", out=G2t[:], in_=Gt[:], func=AF.Square), reads=["Gt"], writes=["G2t"])
        S.op("dve", I("tensor_scalar", out=G2t[:], in0=G2t[:], scalar1=0.044715, scalar2=1.0, op0=ALU.mult, op1=ALU.add),
             reads=["G2t"], writes=["G2t"])
        S.op("dve", I("tensor_tensor", out=G2t[:], in0=G2t[:], in1=Gt[:], op=ALU.mult), reads=["G2t", "Gt"], writes=["G2t"])
        S.op("act", I("activation", out=G2t[:], in_=G2t[:], func=AF.Tanh, scale=0.7978845608028654), reads=["G2t"], writes=["G2t"])
        S.op("dve", I("scalar_tensor_tensor", out=G2t[:], in0=G2t[:], scalar=1.0, in1=Gt[:], op0=ALU.add, op1=ALU.mult),
             reads=["G2t", "Gt"], writes=["G2t"])
        S.op("dve", I("scalar_tensor_tensor", out=yl[:, 0:NOWN], in0=G2t[:, 0:NOWN], scalar=0.5, in1=Hh[:, 1027:2051],
                      op0=ALU.mult, op1=ALU.mult), reads=["G2t", "Hh"], writes=["yl"])
        S.op("dve", I("scalar_tensor_tensor", out=yl[:, NOWN:NOT].rearrange("p (g s) -> p g s", s=TS),
                      in0=G2t[:, NOWN:NOT].rearrange("p (g s) -> p g s", s=TS), scalar=0.5,
                      in1=bcast(Hh[:, SP0 + 3:SP0 + 4], [[11, NSEQ], [1, TS]]), op0=ALU.mult, op1=ALU.mult),
             reads=["G2t", "Hh"], writes=["yl"])
        S.op("sp", I("dma_start", out=yT_s.ap()[c], in_=yl[:]), reads=["yl"], writes=[("yT_s", c)], dma="yts")
    S.op("sp", I("dma_start", out=lruh_o.ap(), in_=lruh[:]), reads=["lruh"], dma="o_lh")
    S.op("sp", I("dma_start", out=lruc_o.ap(), in_=lruc[:]), reads=["lruc"], dma="o_lc")
    S.barrier()
    A.release(m_phase1)
    if stop_after == 1:
        S.final_wait("sp")
        S.emit()
        return nc, S, A
    sS_h = din("sS", [NSEQ, 128, NPAIR, 64])
    SP_o = dout("S_p", [128, NPAIR, 64])
    SS_o = dout("S_s", [NSEQ, 128, NPAIR, 64])
    C.o_s = o_s = dscr("o_s", [2, NPAIR, EW, 64], BF16)

    m_phase2 = A.mark()
    SC = 256
    L1b = [A.alloc(f"L1b{i}", [128, NPAIR, SC + 1, 4], BF16) for i in range(2)]
    Gb = [A.alloc(f"Gb{i}", [128, NPAIR, SC + 1], F32) for i in range(2)]
    LT2 = [A.alloc(f"LT2_{i}", [34, NPAIR, 9, 128], BF16) for i in range(2)]
    R2 = [A.alloc(f"R2_{i}", [34, NPAIR, 9, 64], BF16) for i in range(2)]
    Hin = [A.alloc(f"Hin{i}", [128, NPAIR, 64], F32) for i in range(2)]
    Hr = [A.alloc(f"Hr{i}", [128, NPAIR, 64], F32) for i in range(2)]
    Hz = A.alloc("Hz", [128, NPAIR, 64], F32)
    Hbf = A.alloc("Hbf", [128, NPAIR, 64], BF16)
    S.op("pool", I("memset", Hz[:], 0.0), writes=["Hz"])
    for i in range(2):
        S.op("pool", I("memset", LT2[i][:], 0.0), writes=[("LT2b", i), ("LT2k", i)])
        S.op("pool", I("memset", R2[i][:], 0.0), writes=[("R2u", i, g_) for g_ in range(4)] + [("R2v", i)])

    def load_super(sci):
        sc0 = sci * SC
        b = sci % 2
        lo = max(sc0 - 1, 0)
        col0 = lo - (sc0 - 1)
        n = sc0 + SC - lo
        S.op("sp", I("dma_start", out=L1b[b][:, :, col0:col0 + n, :], in_=L1_s.ap()[:, :, lo:lo + n, :]),
             reads=["L1_s"], writes=[("L1b", b)], dma=f"l1b{b}")
        S.op("sp", I("dma_start", out=Gb[b][:, :, col0:col0 + n], in_=w_s.ap()[:, :, lo:lo + n]),
             reads=["w_s"], writes=[("Gb", b)], dma=f"gb{b}")

    groups = []
    for q in range(256):
        groups.append((8 * q, 8, q == 255, "p", q))
    for g in range(NSEQ):
        groups.append((SB0 + 9 * g, 8, True, "s", g))

    if C.debug and getattr(build, "only_groups", None) is not None:
        groups = [groups[i] for i in build.only_groups]
    C.dbg_tiles = dict(R2=R2, LT2=LT2, Hbf=Hbf, Hin=Hin, Hr=Hr, L1b=L1b, Gb=Gb)

    def load_rows(gi):
        e0, nst, fl, kind, idx = groups[gi]
        sl = gi % 2
        S.op("sp", I("dma_start", out=LT2[sl][0:2, :, 0:nst, :], in_=bk_s.ap()[0, :, :, e0:e0 + nst, :]),
             reads=["bk_s"], writes=[("LT2b", sl)], dma=f"rb{sl}")
        S.op("sp", I("dma_start", out=LT2[sl][32:34, :, 0:nst, :], in_=bk_s.ap()[1, :, :, e0:e0 + nst, :]),
             reads=["bk_s"], writes=[("LT2k", sl)], dma=f"rk{sl}")
        S.op("sp", I("dma_start", out=R2[sl][32:34, :, 0:nst, :], in_=v_s.ap()[:, :, e0:e0 + nst, :]),
             reads=["v_s"], writes=[("R2v", sl)], dma=f"rv{sl}")

    NG = 4
    GP = NPAIR // NG
    P_Hg = [PS[0], PS[1], PS[2], PS[3]]
    P_Ug = [PS[4], PS[5], PS[6], PS[7]]

    def gsl(g):
        return slice(GP * g, GP * g + GP)

    def set_state(g, src_tile, src_key):
        fns = [I("matmul", P_Hg[g][:, 64 * q:64 * q + 64], lhsT=ident, rhs=src_tile[:, GP * g + q, :], start=(q == 0), stop=True,
                 skip_group_check=True) for q in range(GP)]
        S.op("pe", fns, reads=["cmat", src_key], writes=[("psH", g)])
        S.op("dve", I("tensor_copy", out=Hbf[:, gsl(g), :], in_=src_tile[:, gsl(g), :]), reads=[src_key], writes=[("Hbf", g)])

    hr_rot = [0] * NG

    def renorm(g, e):
        b = (e // SC) % 2
        col = (e - 1) - ((e // SC) * SC - 1)
        i = hr_rot[g] % 2
        hr_rot[g] += 1
        g_ap = bcast(Gb[b][:, GP * g, col:col + 1], [[SC + 1, GP], [0, 64]])
        S.op("dve", I("tensor_tensor", out=Hr[i][:, gsl(g), :], in0=P_Hg[g][:, 0:64 * GP].rearrange("p (a b) -> p a b", b=64), in1=g_ap, op=ALU.mult),
             reads=[("psH", g), ("Gb", b)], writes=[("Hr", i, g)])
        return Hr[i], ("Hr", i, g)

    load_super(0)
    load_rows(0)
    next_super = 1
    for gi, (e0, nst, fl, kind, idx) in enumerate(groups):
        sl = gi % 2
        if gi + 1 < len(groups):
            load_rows(gi + 1)
        last_e = e0 + nst
        while next_super <= (e0 // SC) + 1 and next_super * SC < EW - SC + 1:
            load_super(next_super)
            next_super += 1
        nxt = groups[gi + 1] if gi + 1 < len(groups) else None
        if nxt is not None and nxt[3] == "s":
            hi = nxt[4] % 2
            S.op("sp", I("dma_start", out=Hin[hi][:], in_=sS_h.ap()[nxt[4]]), writes=[("Hin", hi, g_) for g_ in range(NG)], dma=f"hin{hi}")
        nent = nst + (1 if fl else 0)
        for s in range(nent):
            e = e0 + s
            b = (e // SC) % 2
            col = e - ((e // SC) * SC - 1)
            is_flush = s == nst
            seq_start = (s == 0) and (kind == "s" or e == 0)
            for g in range(NG):
                if seq_start:
                    if kind == "p":
                        set_state(g, Hz, "Hz")
                    else:
                        set_state(g, Hin[idx % 2], ("Hin", idx % 2, g))
                elif is_flush or (kind == "p" and e % CH == 0):
                    ht, hk = renorm(g, e)
                    if is_flush:
                        dst = SP_o.ap() if kind == "p" else SS_o.ap()[idx]
                        S.op("sp", I("dma_start", out=dst[:, gsl(g), :], in_=ht[:, gsl(g), :]), reads=[hk], dma=f"so{g}")
                    set_state(g, ht, hk)
            for g in range(NG):
                fns = [I("matmul", P_Ug[g][0:4, 64 * q:64 * q + 64], lhsT=L1b[b][:, GP * g + q, col, :], rhs=Hbf[:, GP * g + q, :], start=True, stop=True)
                       for q in range(GP)]
                S.op("pe", fns, reads=[("L1b", b), ("Hbf", g)], writes=[("psU", g)])
            for g in range(NG):
                S.op("act", I("activation", out=R2[sl][0:4, gsl(g), s, :], in_=P_Ug[g][0:4, 0:64 * GP].rearrange("p (a b) -> p a b", b=64), func=AF.Copy),
                     reads=[("psU", g)], writes=[("R2u", sl, g)])
            if not is_flush:
                for g in range(NG):
                    fns = [I("matmul", P_Hg[g][:, 64 * q:64 * q + 64], lhsT=LT2[sl][0:34, GP * g + q, s, :], rhs=R2[sl][0:34, GP * g + q, s, :],
                             start=False, stop=True, skip_group_check=True) for q in range(GP)]
                    S.op("pe", fns, reads=[("LT2b", sl), ("LT2k", sl), ("R2u", sl, g), ("R2v", sl)], writes=[("psH", g)])
                for g in range(NG):
                    S.op("dve", I("tensor_copy", out=Hbf[:, gsl(g), :], in_=P_Hg[g][:, 0:64 * GP].rearrange("p (a b) -> p a b", b=64)),
                         reads=[("psH", g)], writes=[("Hbf", g)])
        S.op("sp", I("dma_start", out=o_s.ap()[:, :, e0:e0 + nent, :], in_=R2[sl][2:4, :, 0:nent, :]),
             reads=[("R2u", sl, g_) for g_ in range(NG)], writes=["o_s"], dma=f"os{sl}")
    S.barrier()
    A.release(m_phase2)
    if stop_after == 2:
        S.final_wait("sp")
        S.emit()
        return nc, S, A
    memT_h = din("memT", [D, 256])
    wk_h = din("wk_t", [16, 128, KC, 128])
    wv_h = din("wv_r", [4, 128, KC, 512])
    wout_h = din("wout_t", [16, 128, KC, 128])
    wq_h = din("wq_t", [16, 128, KC, 128])
    wo_h = din("wo_t", [16, 128, KC, 128])
    w1_h = din("w1_t", [64, 128, KC, 128])
    w2_h = din("w2_t", [16, 4, 128, KC, 128])
    cKT_h = din("cKT", [NSEQ, 128, 16, 256])
    cV_h = din("cV", [NSEQ, 256, D])
    yT_o = dout("yT", [D, NOT])
    mkT_o = dout("mkT", [D, 256])
    mv_o = dout("mv", [256, D])

    onesb = A.alloc("onesb", [128, 128], BF16)
    S.op("dve", I("tensor_copy", out=onesb[:], in_=ones), reads=["cmat"], writes=["onesb"])
    KTp = A.alloc("KTp", [128, 16, 256], BF16)
    Vp = A.alloc("Vp", [128, 2, D], BF16)
    ATT_SCALE = 512.0 ** -0.5

    class WStream:
        def __init__(self, name, nslots, shape):
            self.name, self.n = name, nslots
            self.slots = [A.alloc(f"{name}{i}", shape, BF16) for i in range(nslots)]
            self.queue, self.issued, self.used = [], 0, 0

        def extend(self, aps):
            self.queue.extend(aps)

        def _issue(self):
            if self.issued < len(self.queue):
                sl = self.issued % self.n
                S.op("pool", I("dma_start", out=self.slots[sl][:], in_=self.queue[self.issued]),
                     writes=[(self.name, sl)], dma=f"{self.name}{sl}")
                self.issued += 1

        def get(self, k=1):
            assert k <= self.n
            while self.issued < min(len(self.queue), self.used + self.n):
                self._issue()
            out = []
            for _ in range(k):
                sl = self.used % self.n
                self.used += 1
                out.append((self.slots[sl], (self.name, sl)))
            return out

        def next(self):
            return self.get(1)[0]

    m3a = A.mark()
    memb = A.alloc("memb", [128, KC, 256], BF16)
    S.op("pool", I("dma_start", out=memb[:], in_=memT_h.ap().rearrange("(kc kp) m -> kp kc m", kp=128)), writes=["memb"], dma="c3")
    ws_a = WStream("wa", 4, [128, KC, 128])
    ws_a.extend([wk_h.ap()[i] for i in range(16)])
    kst = [A.alloc(f"kst{i}", [128, 256], F32) for i in range(2)]
    for ft in range(16):
        wt, wk_ = ws_a.next()
        pt = PS[ft % 2]
        S.op("pe", [I("matmul", pt[:, 0:256], lhsT=wt[:, kc, :], rhs=memb[:, kc, :], start=(kc == 0), stop=(kc == KC - 1)) for kc in range(KC)],
             reads=[wk_, "memb"], writes=[("ps", ft % 2)])
        S.op("act", I("activation", out=KTp[:, ft, :], in_=pt[:, 0:256], func=AF.Copy), reads=[("ps", ft % 2)], writes=["KTp"])
        S.op("dve", I("tensor_copy", out=kst[ft % 2][:], in_=pt[:, 0:256]), reads=[("ps", ft % 2)], writes=[("kst", ft % 2)])
        S.op("sp", I("dma_start", out=mkT_o.ap()[128 * ft:128 * ft + 128, :], in_=kst[ft % 2][:]), reads=[("kst", ft % 2)], dma=f"mk{ft % 2}")
    wvs = [A.alloc(f"wvs{i}", [128, KC, 512], BF16) for i in range(2)]
    vst = [A.alloc(f"vst{i}", [128, 512], F32) for i in range(2)]
    for fb in range(4):
        S.op("pool", I("dma_start", out=wvs[fb % 2][:], in_=wv_h.ap()[fb]), writes=[("wvs", fb % 2)], dma=f"wvs{fb % 2}")
        for mc in range(2):
            j = fb * 2 + mc
            pt = PS[2 + j % 2]
            S.op("pe", [I("matmul", pt[:, 0:512], lhsT=memb[:, kc, 128 * mc:128 * mc + 128], rhs=wvs[fb % 2][:, kc, :],
                          start=(kc == 0), stop=(kc == KC - 1)) for kc in range(KC)],
                 reads=[("wvs", fb % 2), "memb"], writes=[("ps", 2 + j % 2)])
            S.op("act", I("activation", out=Vp[:, mc, 512 * fb:512 * fb + 512], in_=pt[:, 0:512], func=AF.Copy), reads=[("ps", 2 + j % 2)], writes=["Vp"])
            S.op("dve", I("tensor_copy", out=vst[j % 2][:], in_=pt[:, 0:512]), reads=[("ps", 2 + j % 2)], writes=[("vst", j % 2)])
            S.op("sp", I("dma_start", out=mv_o.ap()[128 * mc:128 * mc + 128, 512 * fb:512 * fb + 512], in_=vst[j % 2][:]),
                 reads=[("vst", j % 2)], dma=f"mv{j % 2}")
    S.barrier()
    A.release(m3a)
    if stop_after == 2.5:
        S.final_wait("sp")
        S.emit()
        return nc, S, A

    m3b = A.mark()
    yrw = A.alloc("yrw", [128, NPAIR, NOT], BF16)
    Ot = [A.alloc(f"Ot{i}", [128, NPAIR, 2, 64], BF16) for i in range(2)]
    Of = A.alloc("Of", [128, 16, 64], F32)
    Osq = A.alloc("Osq", [128, 16, 64], F32)
    st1 = A.alloc("st1", [128, 16], F32)
    st2 = A.alloc("st2", [128, 16], F32)
    st3 = A.alloc("st3", [128, 16], F32)
    Gt1 = [A.alloc(f"Gt1_{i}", [128, NPAIR, 144], F32) for i in range(2)]
    Gt2 = [A.alloc(f"Gt2_{i}", [128, NPAIR, 144], F32) for i in range(2)]
    ytmp = A.alloc("ytmp", [128, 128], F32)
    def load_tile_3b(tt):
        sl = tt % 2
        if tt < 8:
            ec0 = 1025 + 128 * tt
            for hh in range(2):
                S.op("sp", I("dma_start", out=Ot[sl][:, :, hh, :], in_=o_s.ap()[hh, :, ec0:ec0 + 128, :].rearrange("p e i -> e p i")),
                     reads=["o_s"], writes=[("Ot", sl)], dma=f"ot{sl}")
            S.op("sp", I("dma_start", out=Gt1[sl][:, :, 0:128], in_=G_s.ap()[0, :, :, ec0 - 1:ec0 + 127]), reads=["G_s"], writes=[("Gt1", sl)], dma=f"gt1{sl}")
            S.op("sp", I("dma_start", out=Gt2[sl][:, :, 0:128], in_=G_s.ap()[1, :, :, ec0 - 1:ec0 + 127]), reads=["G_s"], writes=[("Gt2", sl)], dma=f"gt2{sl}")
        else:
            for g in range(NSEQ):
                ecg = SB0 + 9 * g + 1
                for hh in range(2):
                    S.op("sp", I("dma_start", out=Ot[sl][8 * g:8 * g + 8, :, hh, :], in_=o_s.ap()[hh, :, ecg:ecg + 8, :].rearrange("p e i -> e p i")),
                         reads=["o_s"], writes=[("Ot", sl)], dma=f"ot{sl}")
            S.op("sp", I("dma_start", out=Gt1[sl][:, :, 0:144], in_=G_s.ap()[0, :, :, SB0:SB0 + 144]), reads=["G_s"], writes=[("Gt1", sl)], dma=f"gt1{sl}")
            S.op("sp", I("dma_start", out=Gt2[sl][:, :, 0:144], in_=G_s.ap()[1, :, :, SB0:SB0 + 144]), reads=["G_s"], writes=[("Gt2", sl)], dma=f"gt2{sl}")

    load_tile_3b(0)
    for tt in range(9):
        sl = tt % 2
        if tt + 1 < 9:
            load_tile_3b(tt + 1)
        Ofl = Of[:].rearrange("t h i -> t (h i)")
        S.op("act", I("activation", out=Ofl, in_=Ot[sl][:].rearrange("t p h i -> t (p h i)"), func=AF.Copy), reads=[("Ot", sl)], writes=["Of"])
        S.op("dve", I("tensor_reduce", out=st1[:], in_=Of[:], axis=mybir.AxisListType.X, op=ALU.add), reads=["Of"], writes=["st1"])
        S.op("act", I("activation", out=Osq[:].rearrange("t h i -> t (h i)"), in_=Ofl, func=AF.Square), reads=["Of"], writes=["Osq"])
        S.op("dve", I("tensor_reduce", out=st2[:], in_=Osq[:], axis=mybir.AxisListType.X, op=ALU.add), reads=["Osq"], writes=["st2"])
        S.op("dve", I("tensor_scalar", out=st1[:], in0=st1[:], scalar1=1.0 / 64, scalar2=None, op0=ALU.mult), reads=["st1"], writes=["st1"])
        S.op("dve", I("tensor_tensor", out=st3[:], in0=st1[:], in1=st1[:], op=ALU.mult), reads=["st1"], writes=["st3"])
        S.op("dve", I("scalar_tensor_tensor", out=st2[:], in0=st2[:], scalar=1.0 / 64, in1=st3[:], op0=ALU.mult, op1=ALU.subtract),
             reads=["st2", "st3"], writes=["st2"])
        S.op("dve", I("tensor_scalar", out=st2[:], in0=st2[:], scalar1=GN_EPS, scalar2=None, op0=ALU.add), reads=["st2"], writes=["st2"])
        S.op("act", I("activation", out=st2[:], in_=st2[:], func=AF.Sqrt), reads=["st2"], writes=["st2"])
        S.op("dve", I("reciprocal", out=st2[:], in_=st2[:]), reads=["st2"], writes=["st2"])
        S.op("dve", I("tensor_tensor", out=Of[:], in0=Of[:], in1=bcast(st1[:, 0:1], [[1, 16], [0, 64]]), op=ALU.subtract), reads=["Of", "st1"], writes=["Of"])
        S.op("dve", I("tensor_tensor", out=Of[:], in0=Of[:], in1=bcast(st2[:, 0:1], [[1, 16], [0, 64]]), op=ALU.mult), reads=["Of", "st2"], writes=["Of"])
        for half4 in range(2):
            pb = PS[4 + half4]
            S.op("pe", [I("transpose", pb[:, 128 * q:128 * q + 128], Of[:, 2 * (4 * half4 + q):2 * (4 * half4 + q) + 2, :].rearrange("t h i -> t (h i)"), ident)
                        for q in range(4)], reads=["Of", "cmat"], writes=[("ps", 4 + half4)])
            for q in range(4):
                p = 4 * half4 + q
                if tt < 8:
                    g1 = Gt1[sl][:, p, 0:128]
                    g2 = Gt2[sl][:, p, 0:128]
                    dst = yrw[:, p, 128 * tt:128 * tt + 128]
                    src = pb[:, 128 * q:128 * q + 128]
                    tmp = ytmp[:]
                else:
                    g1 = bcast(Gt1[sl][:, p, 0:1], [[9, NSEQ], [1, TS]])
                    g2 = bcast(Gt2[sl][:, p, 0:1], [[9, NSEQ], [1, TS]])
                    dst = yrw[:, p, NOWN:NOT].rearrange("c (g s) -> c g s", s=TS)
                    src = pb[:, 128 * q:128 * q + 128].rearrange("c (g s) -> c g s", s=TS)
                    tmp = ytmp[:].rearrange("c (g s) -> c g s", s=TS)
                S.op("dve", I("tensor_tensor", out=tmp, in0=src, in1=g1, op=ALU.mult), reads=[("ps", 4 + half4), ("Gt1", sl)], writes=["ytmp"])
                S.op("dve", I("tensor_tensor", out=dst, in0=tmp, in1=g2, op=ALU.add), reads=["ytmp", ("Gt2", sl)], writes=["yrw"])
    for p in range(NPAIR):
        S.op("sp", I("dma_start", out=yT_s.ap()[8 + p], in_=yrw[:, p, :]), reads=["yrw"], writes=[("yT_s", 8 + p)], dma="yts")
    S.barrier()
    A.release(m3b)
    if stop_after == 3:
        S.final_wait("sp")
        S.emit()
        return nc, S, A

    TB = 576
    SPL = [(0, 288), (288, 288)]
    XA = A.alloc("XA", [128, 16, TB], F32)
    XB = A.alloc("XB", [128, 16, TB], BF16)
    mean_t = A.alloc("mean_t", [128, TB], F32)
    rstd_t = A.alloc("rstd_t", [128, TB], F32)
    lnt = [A.alloc(f"lnt{i}", [128, TB], F32) for i in range(2)]
    r1b = [A.alloc(f"r1b{i}", [128, TB], BF16) for i in range(2)]
    sqb = [A.alloc(f"sqb{i}", [128, TB], BF16) for i in range(2)]
    ws = WStream("w16", 8, [128, KC, 128])
    for blk in range(2):
        ws.extend([wout_h.ap()[i] for i in range(16)] + [wq_h.ap()[i] for i in range(16)] + [wo_h.ap()[i] for i in range(16)]
                  + [w1_h.ap()[i] for i in range(64)] + [w2_h.ap()[i, qq] for i in range(16) for qq in range(4)])
    lin_rot = [0]
    P_S1 = [PS[4], PS[5]]
    P_S2 = [PS[6], PS[7]]

    def linear_resid_ln(tag, stream, nkc, rhs_fn, rhs_keys, resid_fn, g_col, b_col, out_final=None, kpt=KC, rhs_key_fn=None):
        def stats(ft):
            j = ft % 2
            for si, (s0, sn) in enumerate(SPL):
                S.op("pe", I("matmul", P_S1[si][:, 0:sn], lhsT=onesb[:], rhs=r1b[j][:, s0:s0 + sn], start=(ft == 0), stop=(ft == 15)),
                     reads=["onesb", ("r1b", j)], writes=[("ps", 4 + si)])
                S.op("pe", I("matmul", P_S2[si][:, 0:sn], lhsT=onesb[:], rhs=sqb[j][:, s0:s0 + sn], start=(ft == 0), stop=(ft == 15)),
                     reads=["onesb", ("sqb", j)], writes=[("ps", 6 + si)])

        for ft in range(16):
            wts = stream.get(nkc // kpt)
            res_ap, res_keys = resid_fn(ft)
            for si, (s0, sn) in enumerate(SPL):
                b = lin_rot[0] % 3
                lin_rot[0] += 1
                pt = PS[b]
                mms = [I("matmul", pt[:, 0:sn], lhsT=wts[kc // kpt][0][:, kc % kpt, :], rhs=rhs_fn(kc, s0, sn), start=(kc == 0), stop=(kc == nkc - 1))
                       for kc in range(nkc)]
                if ft == 0 and rhs_key_fn is not None:
                    for kc in range(nkc):
                        S.op("pe", mms[kc], reads=[w_[1] for w_ in wts] + [rhs_key_fn(kc)], writes=[("ps", b)])
                else:
                    S.op("pe", mms, reads=[w_[1] for w_ in wts] + rhs_keys, writes=[("ps", b)])
                S.op("dve", I("scalar_tensor_tensor", out=XA[:, ft, s0:s0 + sn], in0=res_ap[:, s0:s0 + sn], scalar=ALPHA, in1=pt[:, 0:sn],
                              op0=ALU.mult, op1=ALU.add), reads=[("ps", b)] + res_keys, writes=[("XA", ft)])
            j = ft % 2
            S.op("act", I("activation", out=r1b[j][:], in_=XA[:, ft, :], func=AF.Copy), reads=[("XA", ft)], writes=[("r1b", j)])
            S.op("act", I("activation", out=sqb[j][:], in_=XA[:, ft, :], func=AF.Square), reads=[("XA", ft)], writes=[("sqb", j)])
            if ft > 0:
                stats(ft - 1)
        stats(15)
        for si, (s0, sn) in enumerate(SPL):
            S.op("act", I("activation", out=mean_t[:, s0:s0 + sn], in_=P_S1[si][:, 0:sn], func=AF.Copy, scale=1.0 / D), reads=[("ps", 4 + si)], writes=["mean_t"])
            S.op("act", I("activation", out=rstd_t[:, s0:s0 + sn], in_=P_S2[si][:, 0:sn], func=AF.Copy, scale=1.0 / D), reads=[("ps", 6 + si)], writes=["rstd_t"])
        S.op("dve", I("tensor_tensor", out=lnt[0][:], in0=mean_t[:], in1=mean_t[:], op=ALU.mult), reads=["mean_t"], writes=[("lnt", 0)])
        S.op("dve", I("tensor_tensor", out=rstd_t[:], in0=rstd_t[:], in1=lnt[0][:], op=ALU.subtract), reads=["rstd_t", ("lnt", 0)], writes=["rstd_t"])
        S.op("dve", I("tensor_scalar", out=rstd_t[:], in0=rstd_t[:], scalar1=LN_EPS, scalar2=None, op0=ALU.add), reads=["rstd_t"], writes=["rstd_t"])
        S.op("act", I("activation", out=rstd_t[:], in_=rstd_t[:], func=AF.Sqrt), reads=["rstd_t"], writes=["rstd_t"])
        S.op("dve", I("reciprocal", out=rstd_t[:], in_=rstd_t[:]), reads=["rstd_t"], writes=["rstd_t"])
        for ft in range(16):
            j = ft % 2
            S.op("dve", I("tensor_tensor", out=lnt[j][:], in0=XA[:, ft, :], in1=mean_t[:], op=ALU.subtract), reads=[("XA", ft), "mean_t"], writes=[("lnt", j)])
            S.op("dve", I("tensor_tensor", out=lnt[j][:], in0=lnt[j][:], in1=rstd_t[:], op=ALU.mult), reads=[("lnt", j), "rstd_t"], writes=[("lnt", j)])
            S.op("act", I("activation", out=XA[:, ft, :], in_=lnt[j][:], func=AF.Identity, scale=vcol(g_col + ft), bias=vcol(b_col + ft)),
                 reads=[("lnt", j), "vec"], writes=[("XA", ft)])
            if out_final is not None:
                out_final(ft)
            else:
                S.op("act", I("activation", out=XB[:, ft, :], in_=XA[:, ft, :], func=AF.Copy), reads=[("XA", ft)], writes=[("XB", ft)])

    if debug:
        dbgT = {nm: dout("dbg_" + nm, [128, 16, NOT], dt) for nm, dt in (("x1", F32), ("q", BF16), ("att", BF16), ("x2", F32))}

    def dump(nm, src, keys, t0):
        if debug:
            S.op("sp", I("dma_start", out=dbgT[nm].ap()[:, :, t0:t0 + TB], in_=src[:]), reads=keys, dma="dbg")

    for blk in range(2):
        t0 = blk * TB
        mA = A.mark()
        yT = A.alloc("yT", [128, 16, TB], BF16)
        S.op("sp", I("dma_start", out=yT[:], in_=yT_s.ap()[:, :, t0:t0 + TB].rearrange("c p t -> p c t")),
             reads=[("yT_s", c) for c in range(16)], writes=["yT"], dma="c0")
        xr = [A.alloc(f"xr{i}", [128, TB], F32) for i in range(2)]

        def resid_x(ft):
            j = ft % 2
            S.op("sp", I("dma_start", out=xr[j][:], in_=xT_h.ap()[128 * ft:128 * ft + 128, NPRE + t0:NPRE + t0 + TB]), writes=[("xr", j)], dma=f"xr{j}")
            return xr[j], [("xr", j)]

        linear_resid_ln("wout", ws, KC, lambda kc, s0, sn: yT[:, kc, s0:s0 + sn], ["yT"], resid_x, V_LN1G, V_LN1B)
        dump("x1", XA, [("XA", k) for k in range(16)], t0)
        S.barrier()
        A.release(mA)
        mB = A.mark()
        qT = A.alloc("qT", [128, 16, TB], BF16)
        attT = A.alloc("attT", [128, 16, TB], BF16)
        for ft in range(16):
            wt, wkey = ws.next()
            for si, (s0, sn) in enumerate(SPL):
                b = lin_rot[0] % 3
                lin_rot[0] += 1
                pt = PS[b]
                mms = [I("matmul", pt[:, 0:sn], lhsT=wt[:, kc, :], rhs=XB[:, kc, s0:s0 + sn], start=(kc == 0), stop=(kc == KC - 1))
                       for kc in range(KC)]
                if ft == 0:
                    for kc in range(KC):
                        S.op("pe", mms[kc], reads=[wkey, ("XB", kc)], writes=[("ps", b)])
                else:
                    S.op("pe", mms, reads=[wkey] + [("XB", kc) for kc in range(16)], writes=[("ps", b)])
                S.op("act", I("activation", out=qT[:, ft, s0:s0 + sn], in_=pt[:, 0:sn], func=AF.Copy), reads=[("ps", b)], writes=[("qT", ft)])
        pcols = [(0, 288), (288, 288)] if blk == 0 else [(0, 224), (224, 224)]
        PT = [A.alloc(f"PT{i}", [128, 2, 288], BF16) for i in range(2)]
        rden = [A.alloc(f"rden{i}", [128, 288], F32) for i in range(2)]
        it = 0
        for h in range(4):
            for (c0, cn) in pcols:
                j = it % 2
                it += 1
                for mc in range(2):
                    pt = PS[3 + mc]
                    S.op("pe", [I("matmul", pt[:, 0:cn], lhsT=KTp[:, 4 * h + dc, 128 * mc:128 * mc + 128], rhs=qT[:, 4 * h + dc, c0:c0 + cn],
                                  start=(dc == 0), stop=(dc == 3)) for dc in range(4)],
                         reads=["KTp"] + [("qT", 4 * h + dc) for dc in range(4)], writes=[("ps", 3 + mc)])
                    S.op("act", I("activation", out=PT[j][:, mc, 0:cn], in_=pt[:, 0:cn], func=AF.Exp, scale=ATT_SCALE), reads=[("ps", 3 + mc)], writes=[("PT", j)])
                S.op("pe", [I("matmul", PS[5][:, 0:cn], lhsT=onesb[:], rhs=PT[j][:, mc, 0:cn], start=(mc == 0), stop=(mc == 1)) for mc in range(2)],
                     reads=["onesb", ("PT", j)], writes=[("ps", 5)])
                S.op("dve", I("reciprocal", out=rden[j][:, 0:cn], in_=PS[5][:, 0:cn]), reads=[("ps", 5)], writes=[("rden", j)])
                for dc in range(4):
                    pt = PS[6 + dc % 2]
                    S.op("pe", [I("matmul", pt[:, 0:cn], lhsT=Vp[:, mc, 128 * (4 * h + dc):128 * (4 * h + dc) + 128], rhs=PT[j][:, mc, 0:cn],
                                  start=(mc == 0), stop=(mc == 1)) for mc in range(2)], reads=["Vp", ("PT", j)], writes=[("ps", 6 + dc % 2)])
                    S.op("dve", I("tensor_tensor", out=attT[:, 4 * h + dc, c0:c0 + cn], in0=pt[:, 0:cn], in1=rden[j][:, 0:cn], op=ALU.mult),
                         reads=[("ps", 6 + dc % 2), ("rden", j)], writes=[("attT", 4 * h + dc)])
        if blk == 1:
            sc0 = 448
            KTs = [A.alloc(f"KTs{i}", [128, 16, 256], BF16) for i in range(2)]
            Vs = [A.alloc(f"Vs{i}", [128, 2, D], BF16) for i in range(2)]
            PTs = A.alloc("PTs", [128, 4, 2, 128], BF16)
            rdens = A.alloc("rdens", [128, 4, 128], F32)
            PSA = [(PS[6], ("ps", 6)), (PS[7], ("ps", 7)), (PS[0], ("ps", 0)), (PS[1], ("ps", 1))]

            def load_kv(g):
                S.op("pool", I("dma_start", out=KTs[g % 2][:], in_=cKT_h.ap()[g]), writes=[("KTs", g % 2)], dma=f"kts{g % 2}")
                S.op("pool", I("dma_start", out=Vs[g % 2][:], in_=cV_h.ap()[g].rearrange("(mc mp) f -> mp mc f", mp=128)),
                     writes=[("Vs", g % 2)], dma=f"vs{g % 2}")

            load_kv(0)
            for g in range(NSEQ):
                if g + 1 < NSEQ:
                    load_kv(g + 1)
                kt = KTs[g % 2]
                vt = Vs[g % 2]
                for h in range(4):
                    for mc in range(2):
                        bank = PS[3 + (h // 2)]
                        col = ((h % 2) * 2 + mc) * 128 + 8 * g
                        S.op("pe", [I("matmul", bank[:, col:col + 8], lhsT=kt[:, 4 * h + dc, 128 * mc:128 * mc + 128],
                                      rhs=qT[:, 4 * h + dc, sc0 + 8 * g:sc0 + 8 * g + 8], start=(dc == 0), stop=(dc == 3), skip_group_check=True)
                                    for dc in range(4)], reads=[("KTs", g % 2)] + [("qT", 4 * h + dc) for dc in range(4)], writes=[("ps", 3 + h // 2)])
                for hb in range(2):
                    src = bcast(PS[3 + hb][:, 8 * g:8 * g + 1], [[128, 4], [1, 8]])
                    dst = bcast(PTs[:, 2 * hb, 0, 8 * g:8 * g + 1], [[128, 4], [1, 8]])
                    S.op("act", I("activation", out=dst, in_=src, func=AF.Exp, scale=ATT_SCALE), reads=[("ps", 3 + hb)], writes=["PTs"])
                for h in range(4):
                    S.op("pe", [I("matmul", PS[5][:, 128 * h + 8 * g:128 * h + 8 * g + 8], lhsT=onesb[:], rhs=PTs[:, h, mc, 8 * g:8 * g + 8],
                                  start=(mc == 0), stop=(mc == 1), skip_group_check=True) for mc in range(2)],
                         reads=["onesb", "PTs"], writes=[("ps", 5)])
                    pa, pak = PSA[h]
                    for dc in range(4):
                        S.op("pe", [I("matmul", pa[:, 128 * dc + 8 * g:128 * dc + 8 * g + 8], lhsT=vt[:, mc, 128 * (4 * h + dc):128 * (4 * h + dc) + 128],
                                      rhs=PTs[:, h, mc, 8 * g:8 * g + 8], start=(mc == 0), stop=(mc == 1), skip_group_check=True) for mc in range(2)],
                             reads=[("Vs", g % 2), "PTs"], writes=[pak])
            S.op("dve", I("reciprocal", out=rdens[:].rearrange("m h c -> m (h c)"), in_=PS[5][:, 0:512]), reads=[("ps", 5)], writes=["rdens"])
            for h in range(4):
                pa, pak = PSA[h]
                for dc in range(4):
                    S.op("dve", I("tensor_tensor", out=attT[:, 4 * h + dc, sc0:sc0 + 128], in0=pa[:, 128 * dc:128 * dc + 128], in1=rdens[:, h, :], op=ALU.mult),
                         reads=[pak, "rdens"], writes=[("attT", 4 * h + dc)])
        dump("q", qT, [("qT", k) for k in range(16)], t0)
        dump("att", attT, [("attT", k) for k in range(16)], t0)
        linear_resid_ln("wo", ws, KC, lambda kc, s0, sn: attT[:, kc, s0:s0 + sn], [("attT", k) for k in range(16)],
                        lambda ft: (XA[:, ft, :], [("XA", ft)]), V_LN2G, V_LN2B)
        dump("x2", XA, [("XA", k) for k in range(16)], t0)
        S.barrier()
        A.release(mB)
        mC = A.mark()
        hT = A.alloc("hT", [128, 64, TB], BF16)
        hr = [A.alloc(f"hr{i}", [128, 288], F32) for i in range(3)]
        for hc in range(64):
            wt, wkey = ws.next()
            for si, (s0, sn) in enumerate(SPL):
                b = lin_rot[0] % 3
                lin_rot[0] += 1
                pt = PS[b]
                mms = [I("matmul", pt[:, 0:sn], lhsT=wt[:, kc, :], rhs=XB[:, kc, s0:s0 + sn], start=(kc == 0), stop=(kc == KC - 1))
                       for kc in range(KC)]
                if hc == 0:
                    for kc in range(KC):
                        S.op("pe", mms[kc], reads=[wkey, ("XB", kc)], writes=[("ps", b)])
                else:
                    S.op("pe", mms, reads=[wkey] + [("XB", kc) for kc in range(16)], writes=[("ps", b)])
                S.op("act", I("activation", out=hr[b][:, 0:sn], in_=pt[:, 0:sn], func=AF.Relu), reads=[("ps", b)], writes=[("hr", b)])
                S.op("dve", I("tensor_tensor", out=hT[:, hc, s0:s0 + sn], in0=hr[b][:, 0:sn], in1=hr[b][:, 0:sn], op=ALU.mult),
                     reads=[("hr", b)], writes=[("hT", hc)])

        def final_out(ft, t0=t0):
            S.op("sp", I("dma_start", out=yT_o.ap()[128 * ft:128 * ft + 128, t0:t0 + TB], in_=XA[:, ft, :]), reads=[("XA", ft)], dma=f"yo{ft % 2}")

        linear_resid_ln("w2", ws, 64, lambda kc, s0, sn: hT[:, kc, s0:s0 + sn], [("hT", k) for k in range(64)],
                        lambda ft: (XA[:, ft, :], [("XA", ft)]), V_LN3G, V_LN3B, out_final=final_out, kpt=KC)
        S.barrier()
        A.release(mC)

    S.final_wait("sp")
    S.emit()
    return nc, S, A


def _tile_w(w, ncols_pad=None):
    din, dout = w.shape
    if ncols_pad is not None and ncols_pad > dout:
        w = np.concatenate([w, np.zeros((din, ncols_pad - dout), w.dtype)], axis=1)
        dout = ncols_pad
    return np.ascontiguousarray(w.reshape(din // 128, 128, dout // 128, 128).transpose(2, 1, 0, 3))


def _chunks(v, n):
    return np.ascontiguousarray(v.reshape(n, 128).T)


def make_in_maps(inp, cores=range(8)):
    f32 = np.float32
    x_prompt, x_sample = inp["x_prompt"], inp["x_sample"]
    win_t = _tile_w(inp["w_in"][0], N_WT * 128)
    cmat = np.zeros((128, 384), f32)
    cmat[:, 0:128] = np.eye(128, dtype=f32)
    cmat[0:64, 128:192] = 1.0
    cmat[64:128, 192:256] = 1.0
    cmat[:, 256:384] = 1.0
    mu = np.zeros(27 * 128, f32)
    mu[:3360] = inp["rwkv_mu"][0]
    vec_base = np.zeros((128, NV), f32)
    cw = inp["lru_conv_w"][0]
    for c in range(8):
        for j in range(4):
            vec_base[:, V_CW + c * 4 + j] = cw[j, 128 * c:128 * c + 128]
    vec_base[:, V_CB:V_CB + 8] = _chunks(inp["lru_conv_b"][0], 8)
    vec_base[:, V_BA:V_BA + 8] = _chunks(inp["lru_ba"][0], 8)
    vec_base[:, V_BX:V_BX + 8] = _chunks(inp["lru_bx"][0], 8)
    vec_base[:, V_LL:V_LL + 8] = _chunks(inp["lru_L"][0], 8)
    vec_base[:, V_MU:V_MU + 27] = _chunks(mu, 27)
    vec_base[:, V_W0:V_W0 + 8] = _chunks(inp["rwkv_w0"][0], 8)
    vec_base[:, V_A0:V_A0 + 8] = _chunks(inp["rwkv_a0"][0], 8)
    vec_base[:, V_KK:V_KK + 8] = _chunks(inp["rwkv_k_k"][0], 8)
    vec_base[:, V_KA:V_KA + 8] = _chunks(inp["rwkv_k_a"][0], 8)
    vec_base[:, V_RK:V_RK + 8] = _chunks(inp["rwkv_r_k"][0].reshape(-1), 8)
    vec_base[:, V_LNW:V_LNW + 8] = _chunks(inp["rwkv_ln_w"][0], 8)
    vec_base[:, V_LNB:V_LNB + 8] = _chunks(inp["rwkv_ln_b"][0], 8)
    for nm, col in (("ln1_g", V_LN1G), ("ln1_b", V_LN1B), ("ln2_g", V_LN2G), ("ln2_b", V_LN2B), ("ln3_g", V_LN3G), ("ln3_b", V_LN3B)):
        vec_base[:, col:col + 16] = _chunks(inp[nm][0], 16)
    lora_wa = np.ascontiguousarray(np.concatenate([inp["rwkv_w_up"][0], inp["rwkv_a_up"][0]], axis=0))
    gup = np.ascontiguousarray(inp["rwkv_g_up"][0])
    lru_w = np.ascontiguousarray(np.stack([inp["lru_wa"][0], inp["lru_wx"][0]]))
    wk_t = _tile_w(inp["xa_wk"][0])
    wv_r = np.ascontiguousarray(inp["xa_wv"][0].reshape(KC, 128, 4, 512).transpose(2, 1, 0, 3))
    wout_t = _tile_w(inp["w_out"][0])
    wq_t = _tile_w(inp["xa_wq"][0])
    wo_t = _tile_w(inp["xa_wo"][0])
    w1_t = _tile_w(inp["mlp_w1"][0])
    w2_t = np.ascontiguousarray(_tile_w(inp["mlp_w2"][0]).reshape(16, 128, 4, KC, 128).transpose(0, 2, 1, 3, 4))
    maps = []
    for c in cores:
        k, half = c // 2, c % 2
        xs = x_sample[16 * c:16 * c + 16].reshape(NS, D)
        if half == 0:
            pre = np.zeros((NPRE, D), f32)
            own = x_prompt[k, 0:NOWN]
        else:
            pre = x_prompt[k, 0:NPRE]
            own = x_prompt[k, NPRE:NPRE + NOWN]
        xT = np.ascontiguousarray(np.concatenate([pre, own, xs], axis=0).T)
        vec = vec_base.copy()
        vec[:, V_FLAG] = float(half)
        sh = np.zeros((NSEQ, 27 * 128), f32)
        sh[:, :3360] = inp["state_rwkv_shift"][0, 16 * c:16 * c + 16]
        shiftT = np.ascontiguousarray(sh.reshape(NSEQ, 27, 128).transpose(2, 1, 0))
        cv = inp["state_lru_conv"][0, 16 * c:16 * c + 16]
        convT = np.ascontiguousarray(cv.reshape(NSEQ, 3, 8, 128).transpose(3, 2, 0, 1))
        lh = inp["state_lru_h"][0, 16 * c:16 * c + 16]
        lruhT = np.ascontiguousarray(lh.reshape(NSEQ, 8, 128).transpose(2, 1, 0))
        Ss = inp["state_rwkv_S"][0, 16 * c:16 * c + 16]
        sS = np.ascontiguousarray(Ss.reshape(NSEQ, NPAIR, 2, 64, 64).transpose(0, 2, 4, 1, 3).reshape(NSEQ, 128, NPAIR, 64))
        memT = np.ascontiguousarray(inp["mem_prompt"][k].T)
        ck = inp["cache_mem_k"][0, 16 * c:16 * c + 16]
        cKT = np.ascontiguousarray(ck.reshape(NSEQ, 256, 4, 4, 128).transpose(0, 4, 2, 3, 1).reshape(NSEQ, 128, 16, 256))
        cV = np.ascontiguousarray(inp["cache_mem_v"][0, 16 * c:16 * c + 16].reshape(NSEQ, 256, D))
        maps.append({"xT": xT, "w_in_t": win_t, "vec": vec, "cmat": cmat, "shiftT": shiftT, "convT": convT,
                     "lruhT": lruhT, "lora_wa": lora_wa, "gup": gup, "lru_w": lru_w, "sS": sS,
                     "memT": memT, "wk_t": wk_t, "wv_r": wv_r, "wout_t": wout_t, "wq_t": wq_t, "wo_t": wo_t,
                     "w1_t": w1_t, "w2_t": w2_t, "cKT": cKT, "cV": cV})
    return maps


def assemble_outputs(results, cores=range(8)):
    f32 = np.float32
    y_prompt = np.zeros((4, T_P, D), f32)
    y_sample = np.zeros((128, TS, D), f32)
    mk = np.zeros((1, 4, 256, 4, 512), f32)
    mv = np.zeros((1, 4, 256, 4, 512), f32)
    lh_p = np.zeros((1, 4, LRU_W), f32)
    lc_p = np.zeros((1, 4, 3, LRU_W), f32)
    S_p = np.zeros((1, 4, 16, 64, 64), f32)
    sh_p = np.zeros((1, 4, 3360), f32)
    lh_s = np.zeros((1, 128, LRU_W), f32)
    lc_s = np.zeros((1, 128, 3, LRU_W), f32)
    S_s = np.zeros((1, 128, 16, 64, 64), f32)
    sh_s = np.zeros((1, 128, 3360), f32)
    for r, c in zip(results, cores):
        k, half = c // 2, c % 2
        yT = np.asarray(r["yT"])
        y_prompt[k, half * NOWN:(half + 1) * NOWN] = yT[:, :NOWN].T
        y_sample[16 * c:16 * c + 16] = yT[:, NOWN:].T.reshape(NSEQ, TS, D)
        lruh = np.asarray(r["lruh"]).transpose(1, 0, 2).reshape(LRU_W, 1 + NSEQ)
        lruc = np.asarray(r["lruc"]).transpose(1, 0, 2, 3).reshape(LRU_W, 1 + NSEQ, 3)
        shn = np.asarray(r["shn"]).transpose(1, 0, 2).reshape(27 * 128, 1 + NSEQ)[:3360]
        Ssd = np.asarray(r["S_s"]).reshape(NSEQ, 2, 64, NPAIR, 64).transpose(0, 3, 1, 4, 2).reshape(NSEQ, 16, 64, 64)
        lh_s[0, 16 * c:16 * c + 16] = lruh[:, 1:].T
        lc_s[0, 16 * c:16 * c + 16] = lruc[:, 1:, :].transpose(1, 2, 0)
        S_s[0, 16 * c:16 * c + 16] = Ssd
        sh_s[0, 16 * c:16 * c + 16] = shn[:, 1:].T
        if half == 0:
            mk[0, k] = np.asarray(r["mkT"]).T.reshape(256, 4, 512)
            mv[0, k] = np.asarray(r["mv"]).reshape(256, 4, 512)
        else:
            lh_p[0, k] = lruh[:, 0]
            lc_p[0, k] = lruc[:, 0, :].T
            S_p[0, k] = np.asarray(r["S_p"]).reshape(2, 64, NPAIR, 64).transpose(2, 0, 3, 1).reshape(16, 64, 64)
            sh_p[0, k] = shn[:, 0]
    return (y_prompt, y_sample, mk, mv, lh_p, lc_p, S_p, sh_p, lh_s, lc_s, S_s, sh_s)


def kernel(**inputs):
    inp = {k: np.asarray(v) for k, v in inputs.items()}
    maps = make_in_maps(inp)
    nc, S, A = build()
    res = run_bass_kernel_spmd(nc, maps, core_ids=list(range(8)))
    return assemble_outputs(res.results)
```
